# Optimizing a Trainium2 kernel written in Bass

```python
import jax, jax.numpy as jnp
from jax import lax
import numpy as np

D_MODEL = 1024
BATCH = 8
SEQ = 2048
DEPTH = 1
DEC_BATCH = 128
DEC_SEQ = 1
PAST_LEN = 16384
PAGE_SIZE = 128

RET_HEADS = 4
RET_DK = 256
RET_DV = 512
RET_QK = RET_HEADS * RET_DK
RET_V = RET_HEADS * RET_DV
RET_CHUNK = 128
ROPE_BASE = 10000.0
RW_HEAD = 64
RW_HEADS = D_MODEL // RW_HEAD
RW_C = RW_HEADS * RW_HEAD
RW_DECAY_LORA = 64
RW_AAA_LORA = 64
RW_GN_EPS = 1e-5 * RW_HEAD
N_SHIFT = 3 * RW_C + RW_DECAY_LORA + RW_AAA_LORA
N_COLS = 2 * RET_QK + 2 * RET_V + N_SHIFT + RW_C + 2 * D_MODEL
PLE_DIM = 256
NORM_EPS = 1e-6

kernel_name = 'retention_rwkv7_gated_hybrid_step'


def _rmsnorm(x, g):
    xf = x.astype(jnp.float32)
    y = xf * lax.rsqrt(jnp.mean(xf * xf, axis=-1, keepdims=True) + NORM_EPS)
    return (y * g.astype(jnp.float32)).astype(x.dtype)


def _rope(x, pos):
    half = x.shape[-1] // 2
    inv = ROPE_BASE ** (-jnp.arange(half, dtype=jnp.float32) / half)
    ang = pos[:, None] * inv[None, :]
    cos = jnp.cos(ang)[None, :, None, :]
    sin = jnp.sin(ang)[None, :, None, :]
    x1, x2 = x[..., :half], x[..., half:]
    return jnp.concatenate([x1 * cos - x2 * sin, x2 * cos + x1 * sin], axis=-1)


def _retention(q, k, v, s0, log_g):
    B, H, T, _ = q.shape
    C = RET_CHUNK if T % RET_CHUNK == 0 else T
    NC = T // C
    idx = jnp.arange(C, dtype=jnp.float32)
    rel = idx[:, None] - idx[None, :]
    lg = log_g[:, None, None]
    dmask = jnp.where(rel[None] >= 0, jnp.exp(jnp.maximum(rel, 0.0)[None] * lg), 0.0)
    q_dec = jnp.exp((idx + 1.0)[None, :] * log_g[:, None])[:, :, None]
    k_dec = jnp.exp((C - 1.0 - idx)[None, :] * log_g[:, None])[:, :, None]
    chunk_dec = jnp.exp(C * log_g)[:, None, None]

    def chunk(s, qkv):
        qc, kc, vc = qkv
        inner = jnp.einsum('bhid,bhjd->bhij', qc, kc) * dmask
        o = jnp.einsum('bhij,bhjv->bhiv', inner, vc) + jnp.einsum('bhid,bhdv->bhiv', qc * q_dec, s)
        s = chunk_dec * s + jnp.einsum('bhjd,bhjv->bhdv', kc * k_dec, vc)
        return s, o

    split = lambda t: jnp.moveaxis(t.reshape(B, H, NC, C, t.shape[-1]), 2, 0)
    s, o = lax.scan(chunk, s0, (split(q), split(k), split(v)))
    o = jnp.moveaxis(o, 0, 2).reshape(B, H, T, v.shape[-1])
    return o, s


def _retention_branch(q, k, v, g_a, pos0, s0):
    f32 = jnp.float32
    B, T = q.shape[:2]
    pos = pos0 + jnp.arange(T, dtype=f32)
    q = _rope(q.reshape(B, T, RET_HEADS, RET_DK).astype(f32), pos)
    k = _rope(k.reshape(B, T, RET_HEADS, RET_DK).astype(f32), pos) * (RET_DK ** -0.5)
    v = v.reshape(B, T, RET_HEADS, RET_DV).astype(f32)
    tr = lambda t: jnp.transpose(t, (0, 2, 1, 3))
    log_g = jnp.log(1.0 - jnp.exp2(-5.0 - jnp.arange(RET_HEADS, dtype=f32)))
    o, s = _retention(tr(q), tr(k), tr(v), s0.astype(f32), log_g)
    o = o * lax.rsqrt(jnp.mean(o * o, axis=-1, keepdims=True) + NORM_EPS)
    o = tr(o).reshape(B, T, RET_V)
    return o * jax.nn.silu(g_a.astype(f32)), s


def _rwkv_scan(r, w, k, v, kk, a, s0):
    def step(s, inp):
        r_t, w_t, k_t, v_t, kk_t, a_t = inp
        sk = jnp.einsum('bhvk,bhk->bhv', s, kk_t)
        s = s * w_t[:, :, None, :] - sk[..., None] * (a_t * kk_t)[:, :, None, :] + v_t[..., None] * k_t[:, :, None, :]
        return s, jnp.einsum('bhvk,bhk->bhv', s, r_t)

    xs = tuple(jnp.moveaxis(t, 1, 0) for t in (r, w, k, v, kk, a))
    s, o = lax.scan(step, s0, xs)
    return jnp.moveaxis(o, 0, 1), s


def _rwkv_branch(sh, sh_prev, g_b, mu, w0, w2, a0, a2, k_k, k_a, r_k, ln_w, ln_b, s0):
    f32 = jnp.float32
    cur = sh.astype(f32)
    z = cur + (sh_prev.astype(f32) - cur) * mu.astype(f32)
    r, k, v, wl, al = jnp.split(z, [RW_C, 2 * RW_C, 3 * RW_C, 3 * RW_C + RW_DECAY_LORA], axis=-1)
    w = -jax.nn.softplus(-(w0.astype(f32) + jnp.tanh(wl) @ w2.astype(f32))) - 0.5
    decay = jnp.exp(-jnp.exp(w))
    a = jax.nn.sigmoid(a0.astype(f32) + al @ a2.astype(f32))
    B, T = r.shape[:2]
    hs = lambda t: t.reshape(B, T, RW_HEADS, RW_HEAD)
    kk = hs(k * k_k.astype(f32))
    kk = kk / jnp.maximum(jnp.sqrt(jnp.sum(kk * kk, axis=-1, keepdims=True)), 1e-12)
    k = k * (1.0 + (a - 1.0) * k_a.astype(f32))
    r, k, v, decay, a = hs(r), hs(k), hs(v), hs(decay), hs(a)
    o, s = _rwkv_scan(r, decay, k, v, kk, a, s0.astype(f32))
    mean = jnp.mean(o, axis=-1, keepdims=True)
    var = jnp.mean(jnp.square(o - mean), axis=-1, keepdims=True)
    o = ((o - mean) * lax.rsqrt(var + RW_GN_EPS)).reshape(B, T, RW_C) * ln_w.astype(f32) + ln_b.astype(f32)
    bonus = jnp.sum(r * k * r_k.astype(f32), axis=-1, keepdims=True) * v
    o = o + bonus.reshape(B, T, RW_C)
    return o * jax.nn.silu(g_b.astype(f32)), s


def _layer(x, h_prev, s_ret, s_rw, p, pos0, norm_g, w_in, rw_mu, rw_w0, rw_w2, rw_a0, rw_a2, rw_k_k, rw_k_a, rw_r_k, rw_ln_w, rw_ln_b, w_down_a, w_down_b, w_out, w_ple, ple_norm_g, w_ple_gate):
    h = _rmsnorm(x, norm_g)
    hf = jnp.concatenate([h_prev[:, None, :].astype(h.dtype), h], axis=1)
    p_full = jnp.einsum('btd,dn->btn', hf, w_in)
    P = p_full[:, 1:]
    o1 = RET_QK
    o2 = 2 * RET_QK
    o3 = o2 + RET_V
    o4 = o3 + RET_V
    o5 = o4 + N_SHIFT
    o6 = o5 + RW_C
    o7 = o6 + D_MODEL
    sh_prev = p_full[:, :-1, o4:o5]
    q, k, v, g_a, sh, g_b, m_a, m_b = jnp.split(P, [o1, o2, o3, o4, o5, o6, o7], axis=-1)
    y_a, s_ret_new = _retention_branch(q, k, v, g_a, pos0, s_ret)
    y_b, s_rw_new = _rwkv_branch(sh, sh_prev, g_b, rw_mu, rw_w0, rw_w2, rw_a0, rw_a2, rw_k_k, rw_k_a, rw_r_k, rw_ln_w, rw_ln_b, s_rw)
    merged = jax.nn.sigmoid(m_a.astype(jnp.float32)) * (y_a @ w_down_a) + jax.nn.sigmoid(m_b.astype(jnp.float32)) * (y_b @ w_down_b)
    x = x + (merged @ w_out).astype(x.dtype)
    gate = jax.nn.sigmoid(_rmsnorm(x, ple_norm_g) @ w_ple_gate)
    x = x + ((p @ w_ple) * gate).astype(x.dtype)
    return x, h[:, -1], s_ret_new.astype(s_ret.dtype), s_rw_new.astype(s_rw.dtype)


def setup_inputs(seed: int = 0) -> dict:
    key = jax.random.key(seed)
    ks = jax.random.split(key, 32)
    f32 = jnp.float32
    nrm = lambda k, shape, s: jax.random.normal(k, shape, f32) * s
    L = DEPTH
    return {
        'x_prompt': nrm(ks[0], (BATCH, SEQ, D_MODEL), 1.0),
        'x_sample': nrm(ks[1], (DEC_BATCH, DEC_SEQ, D_MODEL), 1.0),
        'state_ret': nrm(ks[2], (L, DEC_BATCH, RET_HEADS, RET_DK, RET_DV), 1.0),
        'state_rwkv': nrm(ks[3], (L, DEC_BATCH, RW_HEADS, RW_HEAD, RW_HEAD), 1.0),
        'state_shift': nrm(ks[4], (L, DEC_BATCH, D_MODEL), 1.0),
        'p_prompt': nrm(ks[5], (L, BATCH, SEQ, PLE_DIM), 1.0),
        'p_sample': nrm(ks[6], (L, DEC_BATCH, DEC_SEQ, PLE_DIM), 1.0),
        'norm_g': 1.0 + nrm(ks[7], (L, D_MODEL), 0.02),
        'w_in': nrm(ks[8], (L, D_MODEL, N_COLS), D_MODEL ** -0.5),
        'rw_mu': jax.random.uniform(ks[9], (L, N_SHIFT), f32),
        'rw_w0': jax.random.uniform(ks[10], (L, RW_C), f32, -6.0, -0.5),
        'rw_w2': nrm(ks[11], (L, RW_DECAY_LORA, RW_C), 0.1),
        'rw_a0': nrm(ks[12], (L, RW_C), 0.1),
        'rw_a2': nrm(ks[13], (L, RW_AAA_LORA, RW_C), RW_AAA_LORA ** -0.5),
        'rw_k_k': 0.85 + nrm(ks[14], (L, RW_C), 0.05),
        'rw_k_a': 1.0 + nrm(ks[15], (L, RW_C), 0.05),
        'rw_r_k': nrm(ks[16], (L, RW_HEADS, RW_HEAD), 0.1),
        'rw_ln_w': 1.0 + nrm(ks[17], (L, RW_C), 0.02),
        'rw_ln_b': nrm(ks[18], (L, RW_C), 0.02),
        'w_down_a': nrm(ks[19], (L, RET_V, D_MODEL), RET_V ** -0.5),
        'w_down_b': nrm(ks[20], (L, RW_C, D_MODEL), RW_C ** -0.5),
        'w_out': nrm(ks[21], (L, D_MODEL, D_MODEL), D_MODEL ** -0.5),
        'w_ple': nrm(ks[22], (L, PLE_DIM, D_MODEL), PLE_DIM ** -0.5),
        'ple_norm_g': 1.0 + nrm(ks[23], (L, D_MODEL), 0.02),
        'w_ple_gate': nrm(ks[24], (L, D_MODEL, D_MODEL), D_MODEL ** -0.5),
        'final_norm_g': 1.0 + nrm(ks[25], (D_MODEL,), 0.02),
    }


def reference(x_prompt, x_sample, state_ret, state_rwkv, state_shift, p_prompt, p_sample, norm_g, w_in, rw_mu, rw_w0, rw_w2, rw_a0, rw_a2, rw_k_k, rw_k_a, rw_r_k, rw_ln_w, rw_ln_b, w_down_a, w_down_b, w_out, w_ple, ple_norm_g, w_ple_gate, final_norm_g):
    xp, xs = x_prompt, x_sample
    pr_ret, pr_rw, pr_sh, sa_ret, sa_rw, sa_sh = [], [], [], [], [], []
    for i in range(DEPTH):
        lw = (norm_g[i], w_in[i], rw_mu[i], rw_w0[i], rw_w2[i], rw_a0[i], rw_a2[i], rw_k_k[i], rw_k_a[i], rw_r_k[i], rw_ln_w[i], rw_ln_b[i], w_down_a[i], w_down_b[i], w_out[i], w_ple[i], ple_norm_g[i], w_ple_gate[i])
        h0 = jnp.zeros((BATCH, D_MODEL), state_shift.dtype)
        r0 = jnp.zeros((BATCH, RET_HEADS, RET_DK, RET_DV), state_ret.dtype)
        w0s = jnp.zeros((BATCH, RW_HEADS, RW_HEAD, RW_HEAD), state_rwkv.dtype)
        xp, hp, rp, wp = _layer(xp, h0, r0, w0s, p_prompt[i], 0, *lw)
        xs, hs_, rs, ws = _layer(xs, state_shift[i], state_ret[i], state_rwkv[i], p_sample[i], PAST_LEN, *lw)
        pr_ret.append(rp)
        pr_rw.append(wp)
        pr_sh.append(hp.astype(state_shift.dtype))
        sa_ret.append(rs)
        sa_rw.append(ws)
        sa_sh.append(hs_.astype(state_shift.dtype))
    y_prompt = _rmsnorm(xp, final_norm_g)
    y_sample = _rmsnorm(xs, final_norm_g)
    return (y_prompt, y_sample, jnp.stack(pr_ret), jnp.stack(pr_rw), jnp.stack(pr_sh), jnp.stack(sa_ret), jnp.stack(sa_rw), jnp.stack(sa_sh))
```

```python
import contextlib
import numpy as np
import ml_dtypes
import concourse.bass as bass
import concourse.mybir as mybir
from concourse.bass_utils import run_bass_kernel_spmd

F32 = mybir.dt.float32
BF16 = mybir.dt.bfloat16
ALU = mybir.AluOpType
AF = mybir.ActivationFunctionType
AX = mybir.AxisListType

NCORES = 8
D = 1024
T = 2048
NT = 16
NS = 16
PAST = 16384
NCOLS = 12416
O1, O2, O3, O4 = 1024, 2048, 4096, 6144
O5 = O4 + 3200
O6 = O5 + 1024
O7 = O6 + 1024
EPS = 1e-6
GN_EPS = 1e-5 * 64
ENGS = ["sync", "scalar", "vector", "gpsimd", "tensor"]
NDS = 40


class Sched:
    def __init__(self, nc, stack):
        self.nc = nc
        self.ops = {e: [] for e in ENGS}
        self.cnt = {e: 0 for e in ENGS}
        self.sem = {e: stack.enter_context(nc.semaphore("se_" + e)) for e in ENGS}
        self.dsem = [stack.enter_context(nc.semaphore("sd%d" % i)) for i in range(NDS)]
        self.dcnt = [0] * NDS
        self.dnext = 0
        self.lastw = {}
        self.readers = {}
        self.seen = {e: {} for e in ENGS}
        self.rec = None

    def _deps(self, reads, writes):
        deps = []
        for k in reads:
            if k in self.lastw:
                deps.extend(self.lastw[k])
        for k in writes:
            if k in self.lastw:
                deps.extend(self.lastw[k])
            deps.extend(self.readers.get(k, []))
        return deps

    def _commit(self, tok, reads, writes):
        for k in reads:
            self.readers.setdefault(k, []).extend(tok if isinstance(tok, list) else [tok])
        toks = tok if isinstance(tok, list) else [tok]
        for k in writes:
            self.lastw[k] = list(toks)
            self.readers[k] = []

    def _waits(self, eng, deps, extra=()):
        need = {}
        for (sk, val) in list(deps) + list(extra):
            if val > need.get(sk, 0):
                need[sk] = val
        out = []
        for sk, val in need.items():
            if self.seen[eng].get(sk, 0) >= val:
                continue
            self.seen[eng][sk] = val
            out.append((sk, val))
        return out

    def op(self, eng, fn, reads=(), writes=(), cost=1.0):
        if self.rec is not None:
            self.rec.append(("op", eng, fn, list(reads), list(writes), cost))
            return None
        deps = self._deps(reads, writes)
        waits = self._waits(eng, deps)
        self.cnt[eng] += 1
        tok = (("e", eng), self.cnt[eng])
        self.ops[eng].append((waits, fn, ("e", eng)))
        self._commit(tok, reads, writes)
        return tok

    def dma(self, fn, reads=(), writes=(), eng="sync", cost=3.0):
        if self.rec is not None:
            self.rec.append(("dma", eng, fn, list(reads), list(writes), cost))
            return None
        i = self.dnext
        self.dnext = (self.dnext + 1) % NDS
        deps = self._deps(reads, writes)
        extra = [(("d", i), self.dcnt[i])] if self.dcnt[i] else []
        waits = self._waits(eng, deps, extra)
        self.dcnt[i] += 16
        tok = (("d", i), self.dcnt[i])
        self.ops[eng].append((waits, fn, ("d", i)))
        self._commit(tok, reads, writes)
        return tok

    def dma_multi(self, fns, reads=(), writes=(), eng="sync", cost=3.0):
        if self.rec is not None:
            self.rec.append(("dmam", eng, fns, list(reads), list(writes), cost))
            return None
        deps = self._deps(reads, writes)
        toks = []
        for fn in fns:
            i = self.dnext
            self.dnext = (self.dnext + 1) % NDS
            extra = [(("d", i), self.dcnt[i])] if self.dcnt[i] else []
            waits = self._waits(eng, deps, extra)
            self.dcnt[i] += 16
            toks.append((("d", i), self.dcnt[i]))
            self.ops[eng].append((waits, fn, ("d", i)))
        self._commit(toks, reads, writes)
        return toks

    def record(self):
        self.rec = []

    def flush(self):
        rec, self.rec = self.rec, None
        n = len(rec)
        lastw, readers = {}, {}
        succs = [[] for _ in range(n)]
        for i, r in enumerate(rec):
            reads, writes = r[3], r[4]
            ps = set()
            for k in reads:
                if k in lastw:
                    ps.add(lastw[k])
            for k in writes:
                if k in lastw:
                    ps.add(lastw[k])
                ps.update(readers.get(k, ()))
            ps.discard(i)
            for p in ps:
                succs[p].append(i)
            for k in reads:
                readers.setdefault(k, []).append(i)
            for k in writes:
                lastw[k] = i
                readers[k] = []
        prio = [0.0] * n
        for i in range(n - 1, -1, -1):
            m = 0.0
            for s_ in succs[i]:
                if prio[s_] > m:
                    m = prio[s_]
            prio[i] = m + rec[i][5]
        import heapq
        indeg = [0] * n
        for i in range(n):
            for s_ in succs[i]:
                indeg[s_] += 1
        ready = [0.0] * n
        heap = [(0.0, -prio[i], i) for i in range(n) if indeg[i] == 0]
        heapq.heapify(heap)
        eng_free = {}
        order = []
        while heap:
            r, _, i = heapq.heappop(heap)
            kind, eng, fn, reads, writes, cost = rec[i]
            start = max(r, eng_free.get(eng, 0.0))
            if kind == "op":
                fin = start + cost
                eng_free[eng] = fin
            else:
                fin = start + cost
                eng_free[eng] = start + 0.2
            order.append(i)
            for s_ in succs[i]:
                if fin > ready[s_]:
                    ready[s_] = fin
                indeg[s_] -= 1
                if indeg[s_] == 0:
                    heapq.heappush(heap, (ready[s_], -prio[s_], s_))
        assert len(order) == n
        for i in order:
            kind, eng, fn, reads, writes, cost = rec[i]
            if kind == "op":
                self.op(eng, fn, reads, writes)
            elif kind == "dma":
                self.dma(fn, reads, writes, eng=eng)
            else:
                self.dma_multi(fn, reads, writes, eng=eng)

    def barrier(self):
        for e in ENGS:
            waits = []
            for i in range(NDS):
                if self.dcnt[i]:
                    waits.append((("d", i), self.dcnt[i]))
            for e2 in ENGS:
                if self.cnt[e2]:
                    waits.append((("e", e2), self.cnt[e2]))
            waits = self._waits(e, waits)
            self.ops[e].append((waits, None, None))
        self.lastw = {}
        self.readers = {}

    def semof(self, sk):
        return self.sem[sk[1]] if sk[0] == "e" else self.dsem[sk[1]]

    def emit(self, eng_name, eng):
        for waits, fn, sk in self.ops[eng_name]:
            for (wk, val) in waits:
                eng.wait_ge(self.semof(wk), val)
            if fn is None:
                continue
            ins = fn(eng)
            ins.then_inc(self.semof(sk), 16 if sk[0] == "d" else 1)

    def final_wait(self, eng_name):
        waits = []
        for i in range(NDS):
            if self.dcnt[i]:
                waits.append((("d", i), self.dcnt[i]))
        for e in ENGS:
            if self.cnt[e] and e != eng_name:
                waits.append((("e", e), self.cnt[e]))
        self.final = (eng_name, waits)


DEBUG = False
DBG = {}
G_RET = [1.0 - 2.0 ** (-5.0 - h) for h in range(4)]


def build_nc(upto="C"):
    nc = bass.Bass("TRN2", target_bir_lowering=False)

    def din(name, shape, dt=F32):
        return nc.dram_tensor(name, list(shape), dt, kind="ExternalInput").ap()

    def dout(name, shape, dt=F32):
        return nc.dram_tensor(name, list(shape), dt, kind="ExternalOutput").ap()

    def dscr(name, shape, dt=F32):
        return nc.dram_tensor(name, list(shape), dt, kind="Internal").ap()

    x = din("x", [T, D])
    xs = din("xs", [NS, D])
    sshift = din("sshift", [NS, D])
    norm_g = din("norm_g", [1, D])
    w_in = din("w_in", [D, NCOLS])
    sret = din("sret", [NS, 4, 256, 512])
    identb_d = din("identb", [128, 128], BF16)
    identf_d = din("identf", [128, 128])
    ropeq_d = din("ropeq", [NT + 1, 128, 2, 128])
    ropek_d = din("ropek", [NT + 1, 128, 2, 128])
    dec_d = din("dec", [128, 12])
    masks_d = din("masks", [3, 128, 128])
    sel_d = din("sel", [NS, NS, 128])
    eye16_d = din("eye16", [1, 256], BF16)

    rw_mu = din("rw_mu", [1, 3200])
    rw_w0 = din("rw_w0", [1, 1024])
    rw_w2 = din("rw_w2", [64, 1024])
    rw_a0 = din("rw_a0", [1, 1024])
    rw_a2 = din("rw_a2", [64, 1024])
    rw_k_k = din("rw_k_k", [1, 1024])
    rw_k_a = din("rw_k_a", [1, 1024])
    rw_r_k = din("rw_r_k", [1, 1024])
    rw_ln_w = din("rw_ln_w", [1, 1024])
    rw_ln_b = din("rw_ln_b", [1, 1024])
    srw = din("srw", [NS, 16, 64, 64])
    tri_d = din("tri", [2, 128, 128])
    c0col_d = din("c0col", [128, 1])
    mask3_d = din("mask3", [128, 3, 128])
    sel2_d = din("sel2", [NS, 8, 128])
    hm_d = din("hm", [NS, 128])
    Em_d = din("Em", [NS, 8])
    Fm_d = din("Fm", [128, 64])
    hmask_d = din("hmask", [128, 2])
    rwkv_p = dout("rwkv_p", [16, 64, 64])
    rwkv_s = dout("rwkv_s", [NS, 16, 64, 64])
    preB = (dout if DEBUG else dscr)("preB", [NT, 128, 7168], BF16)
    preF = (dout if DEBUG else dscr)("preF", [NT + 1, 128, 1032])
    lastrow_d = dscr("lastrow_d", [1, 3200])
    yb_all = (dout if DEBUG else dscr)("yb_all", [NT + 1, 128, 1024])
    sampre = dscr("sampre", [NS, 7, 1024])

    w_down_a = din("w_down_a", [2048, 1024])
    w_down_b = din("w_down_b", [1024, 1024])
    w_out = din("w_out", [1024, 1024])
    w_ple_gate = din("w_ple_gate", [1024, 1024])
    w_ple = din("w_ple", [256, 1024])
    ple_norm_g = din("ple_norm_g", [1, 1024])
    final_norm_g = din("final_norm_g", [1, 1024])
    pp = din("pp", [T, 256])
    psm = din("psm", [NS, 256])
    y_p = dout("y_p", [T, D])
    y_s = dout("y_s", [NS, D])
    shift_p = dout("shift_p", [1, D])
    shift_s = dout("shift_s", [NS, D])
    ret_p = dout("ret_p", [4, 256, 512])
    ret_s = dout("ret_s", [NS, 4, 256, 512])

    hT_all = dscr("hT_all", [128, 8, T + NS], BF16)
    ssT_d = dscr("ssT_d", [128, 8, NS], BF16)
    yaT_all = dscr("yaT_all", [NT + 1, 128, 16, 128], BF16)

    with contextlib.ExitStack() as st0:
        S = Sched(nc, st0)

        HOP = 0.7
        ENGNS = {"vector": 1.0, "scalar": 1.1, "gpsimd": 2.6}

        def _free(ap):
            n = 1
            for d in ap.shape[1:]:
                n *= d
            return n

        def OP(eng, name, reads, writes, **kw):
            o = kw.get("out", kw.get("ap"))
            cost = HOP + ENGNS.get(eng, 1.0) * _free(o) / 1000.0
            return S.op(eng, lambda e, kw=kw, name=name: getattr(e, name)(**kw), reads, writes, cost=cost)

        def MM(reads, writes, mms):
            def fn(e, mms=mms):
                ins = None
                for m in mms:
                    ins = e.matmul(**m)
                return ins
            cost = HOP
            for m in mms:
                f = 4.0 if m["rhs"].dtype == F32 else 1.0
                cost += f * max(64, _free(m["rhs"])) * 0.00045
            return S.op("tensor", fn, reads, writes, cost=cost)

        def TR(reads, writes, trs):
            def fn(e, trs=trs):
                ins = None
                for m in trs:
                    ins = e.transpose(**m)
                return ins
            cost = HOP
            for m in trs:
                f = 4.0 if m["in_"].dtype == F32 else 1.0
                cost += f * max(64, m["in_"].shape[0]) * 0.00045
            return S.op("tensor", fn, reads, writes, cost=cost)

        def _dcost(out):
            try:
                n = out.shape[0] * _free(out) * (2 if out.dtype == BF16 else 4)
            except Exception:
                n = 1 << 18
            return 2.0 + n / 200e3

        def DMA(out, in_, reads, writes, eng="sync"):
            return S.dma(lambda e, out=out, in_=in_: e.dma_start(out=out, in_=in_), reads, writes, eng=eng, cost=_dcost(out))

        def DMAS(out, in_, reads, writes):
            return DMA(out, in_, reads, writes, eng="sync")

        def mk(st):
            def sb(name, shape, dt=F32):
                return st.enter_context(nc.sbuf_tensor(name, list(shape), dt))

            def ps(name, shape, dt=F32):
                return st.enter_context(nc.psum_tensor(name, list(shape), dt))
            return sb, ps

        sb0, ps0 = mk(st0)
        identb = sb0("identb_sb", [128, 128], BF16)
        identf = sb0("identf_sb", [128, 128])
        DMA(identb[:], identb_d[:, :], [], ["identb"])
        DMA(identf[:], identf_d[:, :], [], ["identf"])

        nhalf = sb0("nhalf", [128, 16])
        S.op("gpsimd", lambda e: e.memset(nhalf[:], -0.5), [], ["nhalf"])

        def rsqrt_chain(buf, key, rows, scale, bias):
            OP("vector", "tensor_scalar", [key], [key], out=buf, in0=buf, scalar1=scale, scalar2=bias,
               op0=ALU.mult, op1=ALU.add)
            OP("gpsimd", "tensor_tensor", [key, "nhalf"], [key], out=buf, in0=buf,
               in1=nhalf[:rows, 0:buf.shape[1]], op=ALU.pow)

        wkeys = {}

        def load_w(dst, src, K, N, stg, name, ci=[0]):
            keys = []
            CH = stg[0].shape[1]
            for kc in range(K // 128):
                for n0 in range(0, N, CH):
                    n1 = min(N, n0 + CH)
                    i = ci[0] % len(stg)
                    DMA(stg[i][:, :n1 - n0], src[kc * 128:(kc + 1) * 128, n0:n1], [], [("stg", i)])
                    eng = ["vector", "scalar", "vector", "scalar", "gpsimd"][ci[0] % 5]
                    k = (name, kc, n0)
                    if eng == "scalar":
                        OP(eng, "copy", [("stg", i)], [k], out=dst[:, kc, n0:n1], in_=stg[i][:, :n1 - n0])
                    else:
                        OP(eng, "tensor_copy", [("stg", i)], [k], out=dst[:, kc, n0:n1], in_=stg[i][:, :n1 - n0])
                    keys.append(k)
                    ci[0] += 1
            wkeys[name] = keys


        if upto >= "A":
          with contextlib.ExitStack() as stA:
            sbA, psA = mk(stA)
            wA = sbA("wA", [128, 8, 6144], BF16)
            dec = sbA("dec_sb", [128, 12])
            maskU = sbA("maskU_sb", [128, 128])
            DMA(dec[:], dec_d[:, :], [], ["dec"])
            DMA(maskU[:], masks_d[0], [], ["maskU"])
            with contextlib.ExitStack() as st:
                sb, ps = mk(st)
                S.record()
                stg = [sb("stgA%d" % i, [128, 2048]) for i in range(6)]
                load_w(wA, w_in[:, 0:6144], D, 6144, stg, "wA")
                gbc = sb("gbc", [128, D])
                DMA(gbc[:], norm_g.partition_broadcast(128), [], ["gbc"])
                xt = [sb("xt%d" % i, [128, D]) for i in range(2)]
                junk = sb("junk0", [128, D])
                ssq = sb("ssq", [128, 2])
                rstd = sb("rstd", [128, 2])
                hf = [sb("hf%d" % i, [128, D]) for i in range(2)]
                hb = [sb("hb%d" % i, [128, D], BF16) for i in range(2)]
                hTs = [sb("hTs%d" % i, [128, 8, 128], BF16) for i in range(2)]
                psT = [ps("psT%d" % i, [128, 8, 128], BF16) for i in range(2)]

                def tile_src(t):
                    if t < NT:
                        return x[t * 128:(t + 1) * 128, :], 128
                    if t == NT:
                        return xs[:, :], NS
                    return sshift[:, :], NS

                for t in range(NT + 2):
                    b = t % 2
                    src, rows = tile_src(t)
                    DMA(xt[b][:rows, :], src, [], [("xt", b)])
                    if t <= NT:
                        OP("vector", "memset", [], [("ssq", b)], ap=ssq[:rows, b:b + 1], constant=0.0)
                        OP("scalar", "activation", [("xt", b)], ["junk0", ("ssq", b)], out=junk[:rows, :],
                           in_=xt[b][:rows, :], func=AF.Square, accum_out=ssq[:rows, b:b + 1])
                        OP("vector", "tensor_copy", [("ssq", b)], [("rstd", b)], out=rstd[:rows, b:b + 1],
                           in_=ssq[:rows, b:b + 1])
                        rsqrt_chain(rstd[:rows, b:b + 1], ("rstd", b), rows, 1.0 / D, EPS)
                        OP("vector", "scalar_tensor_tensor", [("xt", b), ("rstd", b), "gbc"], [("hf", b)],
                           out=hf[b][:rows, :], in0=xt[b][:rows, :], scalar=rstd[:rows, b:b + 1],
                           in1=gbc[:rows, :], op0=ALU.mult, op1=ALU.mult)
                        if t == NT - 1:
                            DMA(shift_p[0:1, :], hf[b][127:128, :], [("hf", b)], ["o_shift_p"])
                        if t == NT:
                            DMA(shift_s[:, :], hf[b][:NS, :], [("hf", b)], ["o_shift_s"])
                        OP("gpsimd", "tensor_copy", [("hf", b)], [("hb", b)], out=hb[b][:rows, :], in_=hf[b][:rows, :])
                    else:
                        OP("gpsimd", "tensor_copy", [("xt", b)], [("hb", b)], out=hb[b][:rows, :], in_=xt[b][:rows, :])
                    TR([("hb", b), "identb"], [("psT", b)],
                       [dict(out=psT[b][:, kc, :rows], in_=hb[b][:rows, kc * 128:(kc + 1) * 128],
                             identity=identb[:rows, :rows]) for kc in range(8)])
                    OP("scalar", "copy", [("psT", b)], [("hTs", b)], out=hTs[b][:, :, :rows], in_=psT[b][:, :, :rows])
                    dst = hT_all[:, :, t * 128:t * 128 + rows] if t <= NT else ssT_d[:, :, :]
                    DMAS(dst, hTs[b][:, :, :rows], [("hTs", b)], [("hT_all", t)])
                S.flush()
            S.barrier()
            WA = ["wA"]

            def projA(hT_, rows, g, outp):
                MM([hT_[1]] + WA, [outp[1]],
                   [dict(out=outp[0], lhsT=hT_[0][:, kc, :rows], rhs=wA[:, kc, g * 512:(g + 1) * 512],
                         start=(kc == 0), stop=(kc == 7)) for kc in range(8)])

            def rope(src_ps, srck, rows, tab, tabk, rot, rotk, tmps):
                sv = src_ps.rearrange("p (h c f) -> p h c f", h=4, c=2)
                rv = rot.rearrange("p (h c f) -> p h c f", h=4, c=2)
                x1, x2 = sv[:, :, 0, :], sv[:, :, 1, :]
                cosb = tab[:, 0, :].unsqueeze(1).to_broadcast([rows, 4, 128])
                sinb = tab[:, 1, :].unsqueeze(1).to_broadcast([rows, 4, 128])
                tv = [(tt[0][:rows, :].rearrange("p (h f) -> p h f", h=4), tt[1]) for tt in tmps]
                OP("vector", "tensor_tensor", [srck, tabk], [tv[0][1]], out=tv[0][0], in0=x1, in1=cosb, op=ALU.mult)
                OP("vector", "tensor_tensor", [srck, tabk], [tv[1][1]], out=tv[1][0], in0=x2, in1=sinb, op=ALU.mult)
                OP("gpsimd", "tensor_tensor", [tv[0][1], tv[1][1]], [rotk], out=rv[:, :, 0, :], in0=tv[0][0],
                   in1=tv[1][0], op=ALU.subtract)
                OP("vector", "tensor_tensor", [srck, tabk], [tv[2][1]], out=tv[2][0], in0=x2, in1=cosb, op=ALU.mult)
                OP("vector", "tensor_tensor", [srck, tabk], [tv[3][1]], out=tv[3][0], in0=x1, in1=sinb, op=ALU.mult)
                OP("gpsimd", "tensor_tensor", [tv[2][1], tv[3][1]], [rotk], out=rv[:, :, 1, :], in0=tv[2][0],
                   in1=tv[3][0], op=ALU.add)

            def head_norm_gate(po_, pok, rows, sg_, sgk, ya_, yak, h, ssh, sshk):
                OP("vector", "memset", [], [sshk], ap=ssh, constant=0.0)
                OP("scalar", "activation", [pok], ["junkA", sshk], out=junkA[:rows, :], in_=po_,
                   func=AF.Square, accum_out=ssh)
                rsqrt_chain(ssh, sshk, rows, 1.0 / 512, EPS)
                OP("vector", "scalar_tensor_tensor", [pok, sshk, sgk], [yak],
                   out=ya_[:rows, h * 512:(h + 1) * 512], in0=po_, scalar=ssh,
                   in1=sg_[:rows, h * 512:(h + 1) * 512], op0=ALU.mult, op1=ALU.mult)

            junkA = sbA("junkA", [128, 512])

            with contextlib.ExitStack() as st:
                sb, ps = mk(st)
                hTt = [sb("hTt%d" % i, [128, 8, 128], BF16) for i in range(2)]
                rq = [sb("rq%d" % i, [128, 2, 128]) for i in range(2)]
                rk = [sb("rk%d" % i, [128, 2, 128]) for i in range(2)]
                rot = sb("rot", [128, 1024])
                tmps = [(sb("ropet%d" % i, [128, 512]), ("ropet", i)) for i in range(4)]
                qd = sb("qd", [128, 1024], BF16)
                kh = sb("kh", [128, 1024], BF16)
                kd2 = [sb("kd%d" % i, [128, 1024], BF16) for i in range(2)]
                qdT2 = [sb("qdT%d" % i, [128, 8, 128], BF16) for i in range(2)]
                khT2 = [sb("khT%d" % i, [128, 8, 128], BF16) for i in range(2)]
                vbf2 = [sb("vbf%d" % i, [128, 2048], BF16) for i in range(2)]
                sg2 = [sb("sg%d" % i, [128, 2048]) for i in range(2)]
                innerm = sb("innerm", [128, 4, 128], BF16)
                ya2 = [sb("ya%d" % i, [128, 2048], BF16) for i in range(2)]
                yT = sb("yT", [128, 16, 128], BF16)
                Sst = sb("Sst", [128, 8, 512])
                Sbf = sb("Sbf", [128, 8, 512], BF16)
                ssh = sb("ssh", [128, 4])
                pq = ps("pq", [128, 1024])
                pv = [ps("pv%d" % i, [128, 512]) for i in range(2)]
                pTf = ps("pTf", [128, 512])
                pT = pTf[:, :].bitcast(BF16).rearrange("p (c f) -> p c f", c=8)
                pin = ps("pin", [128, 512])
                po = [ps("po%d" % i, [128, 512]) for i in range(2)]

                OP("vector", "memset", [], [("Sst", c) for c in range(8)], ap=Sst[:], constant=0.0)
                OP("gpsimd", "memset", [], [("Sbf", c) for c in range(8)], ap=Sbf[:], constant=0.0)

                def loadA(t):
                    b = t % 2
                    DMA(hTt[b][:], hT_all[:, :, t * 128:(t + 1) * 128], [("hT_all", t)], [("hTt", b)])
                    DMA(rq[b][:], ropeq_d[t], [], [("rq", b)])
                    DMA(rk[b][:], ropek_d[t], [], [("rk", b)])

                pvi = [0]
                poi = [0]

                def computeA(t):
                    b = t % 2
                    kd, qdT, khT, vbf, sg, ya = kd2[b], qdT2[b], khT2[b], vbf2[b], sg2[b], ya2[b]
                    KD, QDT, KHT, VBF, SG, YA = ("kd", b), ("qdT", b), ("khT", b), ("vbf", b), ("sg", b), ("ya", b)
                    hT_ = (hTt[b], ("hTt", b))
                    for g in range(2):
                        projA(hT_, 128, g, (pq[:, g * 512:(g + 1) * 512], "pq"))
                    rope(pq[:, :], "pq", 128, rq[b], ("rq", b), rot[:, :], "rot", tmps)
                    OP("gpsimd", "tensor_tensor", ["rot", "dec"], ["qd"],
                       out=qd[:, :].rearrange("p (h f) -> p h f", h=4),
                       in0=rot[:, :].rearrange("p (h f) -> p h f", h=4),
                       in1=dec[:, 0:4].unsqueeze(2).to_broadcast([128, 4, 256]), op=ALU.mult)
                    for g in range(2):
                        projA(hT_, 128, 2 + g, (pq[:, g * 512:(g + 1) * 512], "pq"))
                    rope(pq[:, :], "pq", 128, rk[b], ("rk", b), rot[:, :], "rot", tmps)
                    OP("gpsimd", "tensor_tensor", ["rot", "dec"], ["kh"],
                       out=kh[:, :].rearrange("p (h f) -> p h f", h=4),
                       in0=rot[:, :].rearrange("p (h f) -> p h f", h=4),
                       in1=dec[:, 4:8].unsqueeze(2).to_broadcast([128, 4, 256]), op=ALU.mult)
                    OP("gpsimd", "tensor_tensor", ["rot", "dec"], [KD],
                       out=kd[:, :].rearrange("p (h f) -> p h f", h=4),
                       in0=rot[:, :].rearrange("p (h f) -> p h f", h=4),
                       in1=dec[:, 8:12].unsqueeze(2).to_broadcast([128, 4, 256]), op=ALU.mult)
                    TR(["qd", "identb"], ["pT"],
                       [dict(out=pT[:, c, :], in_=qd[:, c * 128:(c + 1) * 128], identity=identb[:, :]) for c in range(8)])
                    OP("scalar", "copy", ["pT"], [QDT], out=qdT[:], in_=pT)
                    TR(["kh", "identb"], ["pT"],
                       [dict(out=pT[:, c, :], in_=kh[:, c * 128:(c + 1) * 128], identity=identb[:, :]) for c in range(8)])
                    OP("scalar", "copy", ["pT"], [KHT], out=khT[:], in_=pT)
                    for g in range(4):
                        k = pvi[0] % 2
                        pvi[0] += 1
                        projA(hT_, 128, 4 + g, (pv[k][:, :], ("pv", k)))
                        OP("scalar", "copy", [("pv", k)], [VBF], out=vbf[:, g * 512:(g + 1) * 512], in_=pv[k][:, :])
                    for g in range(4):
                        k = pvi[0] % 2
                        pvi[0] += 1
                        projA(hT_, 128, 8 + g, (pv[k][:, :], ("pv", k)))
                        OP("scalar", "activation", [("pv", k)], [SG], out=sg[:, g * 512:(g + 1) * 512],
                           in_=pv[k][:, :], func=AF.Silu)
                    MM([KHT, QDT], ["pin"],
                       [dict(out=pin[:, h * 128:(h + 1) * 128], lhsT=khT[:, 2 * h + c, :], rhs=qdT[:, 2 * h + c, :],
                             start=(c == 0), stop=(c == 1)) for h in range(4) for c in range(2)])
                    OP("vector", "tensor_tensor", ["pin", "maskU"], ["innerm"], out=innerm[:],
                       in0=pin[:, :].rearrange("p (h f) -> p h f", h=4),
                       in1=maskU[:, :].unsqueeze(1).to_broadcast([128, 4, 128]), op=ALU.mult)
                    for h in range(4):
                        k = poi[0] % 2
                        poi[0] += 1
                        MM(["innerm", VBF, QDT, ("Sbf", 2 * h), ("Sbf", 2 * h + 1)], [("po", k)],
                           [dict(out=po[k][:, :], lhsT=innerm[:, h, :], rhs=vbf[:, h * 512:(h + 1) * 512],
                                 start=True, stop=False)] +
                           [dict(out=po[k][:, :], lhsT=qdT[:, 2 * h + c, :], rhs=Sbf[:, 2 * h + c, :],
                                 start=False, stop=(c == 1)) for c in range(2)])
                        head_norm_gate(po[k][:, :], ("po", k), 128, sg, SG, ya, YA, h, ssh[:, h:h + 1], ("ssh", h))
                    for h in range(4):
                        for c in range(2):
                            k = pvi[0] % 2
                            pvi[0] += 1
                            ch = 2 * h + c
                            MM([KD, VBF], [("pv", k)],
                               [dict(out=pv[k][:, :], lhsT=kd[:, ch * 128:(ch + 1) * 128],
                                     rhs=vbf[:, h * 512:(h + 1) * 512], start=True, stop=True)])
                            OP("vector", "scalar_tensor_tensor", [("pv", k), ("Sst", ch)], [("Sst", ch)],
                               out=Sst[:, ch, :], in0=Sst[:, ch, :], scalar=float(G_RET[h] ** 128),
                               in1=pv[k][:, :], op0=ALU.mult, op1=ALU.add)
                            OP("scalar", "copy", [("Sst", ch)], [("Sbf", ch)], out=Sbf[:, ch, :], in_=Sst[:, ch, :])
                    for half in range(2):
                        TR([YA, "identb"], ["pT"],
                           [dict(out=pT[:, c, :], in_=ya[:, (half * 8 + c) * 128:(half * 8 + c + 1) * 128],
                                 identity=identb[:, :]) for c in range(8)])
                        OP("scalar", "copy", ["pT"], ["yT"], out=yT[:, half * 8:(half + 1) * 8, :], in_=pT)
                    DMAS(yaT_all[t], yT[:], ["yT"], [("yaT_all", t)])

                S.record()
                loadA(0)
                for t in range(NT):
                    if t + 1 < NT:
                        loadA(t + 1)
                    computeA(t)
                DMA(ret_p.rearrange("h (c p) v -> p (h c) v", c=2), Sst[:], [("Sst", c) for c in range(8)], ["o_ret_p"])
                S.flush()
            S.barrier()

            with contextlib.ExitStack() as st:
                sb, ps = mk(st)
                hTs_ = sb("hTsA", [128, 8, NS], BF16)
                rq = sb("rqs", [NS, 2, 128])
                rk = sb("rks", [NS, 2, 128])
                rotq = sb("rotq", [NS, 1024])
                rotk = sb("rotk", [NS, 1024])
                tmps = [(sb("ropets%d" % i, [NS, 512]), ("ropet", i)) for i in range(4)]
                qb = sb("qbs", [NS, 1024], BF16)
                qT = sb("qTs", [128, 8, NS], BF16)
                kTf = sb("kTfs", [128, 8, NS])
                vs = sb("vss", [NS, 2048])
                sg = sb("sgs", [NS, 2048])
                ya = sb("yas", [NS, 2048], BF16)
                yT = sb("yTs", [128, 16, NS], BF16)
                ssh = sb("sshs", [NS, 4])
                sel = sb("sel_sb", [NS, NS, 128])
                eye16 = sb("eye16_sb", [128, NS, NS], BF16)
                qmask = sb("qmask", [128, 8, NS, NS], BF16)
                sin_ = [sb("sin%d" % i, [128, 2, 512]) for i in range(4)]
                snew = [sb("snew%d" % i, [128, 2, 512]) for i in range(2)]
                snb = [sb("snb%d" % i, [128, 2, 512], BF16) for i in range(2)]
                pq = ps("pqs", [128, 1024])
                pvb = [ps("pvb%d" % i, [128, 512]) for i in range(2)]
                pos = [ps("pos%d" % i, [128, 512]) for i in range(4)]
                pT = pq[:, 0:512].bitcast(BF16).rearrange("p (c f) -> p c f", c=8)
                pTf = pq[:, :].rearrange("p (c f) -> p c f", c=8)

                S.record()
                DMA(hTs_[:], hT_all[:, :, T:T + NS], [("hT_all", NT)], ["hTsA"])
                DMA(rq[:], ropeq_d[NT, 0:NS], [], ["rqs"])
                DMA(rk[:], ropek_d[NT, 0:NS], [], ["rks"])
                DMA(sel[:], sel_d[:, :, :], [], ["sel"])
                DMA(eye16[:].rearrange("p a b -> p (a b)"), eye16_d.partition_broadcast(128), [], ["eye16"])
                hT_ = (hTs_, "hTsA")
                for g in range(2):
                    projA(hT_, NS, g, (pq[:NS, g * 512:(g + 1) * 512], "pq"))
                rope(pq[:NS, :], "pq", NS, rq, "rqs", rotq[:, :], "rotq", tmps)
                for g in range(2):
                    projA(hT_, NS, 2 + g, (pq[:NS, g * 512:(g + 1) * 512], "pq"))
                rope(pq[:NS, :], "pq", NS, rk, "rks", rotk[:, :], "rotk", tmps)
                OP("gpsimd", "tensor_copy", ["rotq"], ["qbs"], out=qb[:], in_=rotq[:])
                TR(["qbs", "identb"], ["pq"],
                   [dict(out=pT[:, c, :NS], in_=qb[:, c * 128:(c + 1) * 128], identity=identb[:NS, :NS]) for c in range(8)])
                OP("scalar", "copy", ["pq"], ["qTs"], out=qT[:], in_=pT[:, :, :NS])
                TR(["rotk", "identf"], ["pq"],
                   [dict(out=pTf[:, c, :NS], in_=rotk[:, c * 128:(c + 1) * 128], identity=identf[:NS, :NS]) for c in range(8)])
                OP("scalar", "copy", ["pq"], ["kTfs"], out=kTf[:], in_=pTf[:, :, :NS])
                for g in range(4):
                    k = g % 2
                    projA(hT_, NS, 4 + g, (pvb[k][:NS, :], ("pvb", k)))
                    OP("scalar", "copy", [("pvb", k)], ["vss"], out=vs[:, g * 512:(g + 1) * 512], in_=pvb[k][:NS, :])
                for g in range(4):
                    k = g % 2
                    projA(hT_, NS, 8 + g, (pvb[k][:NS, :], ("pvb", k)))
                    OP("scalar", "activation", [("pvb", k)], ["sgs"], out=sg[:, g * 512:(g + 1) * 512],
                       in_=pvb[k][:NS, :], func=AF.Silu)
                OP("vector", "tensor_tensor", ["qTs", "eye16"], ["qmask"], out=qmask[:],
                   in0=qT[:].unsqueeze(2).to_broadcast([128, 8, NS, NS]),
                   in1=eye16[:].unsqueeze(1).to_broadcast([128, 8, NS, NS]), op=ALU.mult)

                def load_s(i):
                    b, h = divmod(i, 4)
                    DMA(sin_[i % 4][:], sret[b, h].rearrange("(c p) v -> p c v", c=2), [], [("sin", i % 4)])

                load_s(0)
                load_s(1)
                load_s(2)

                def bc_v(i):
                    b, h = divmod(i, 4)
                    k = i % 2
                    MM(["sel", "vss"], [("pvb", k)],
                       [dict(out=pvb[k][:, :], lhsT=sel[:, b, :], rhs=vs[:, h * 512:(h + 1) * 512], start=True, stop=True)])

                bc_v(0)
                for i in range(NS * 4):
                    b, h = divmod(i, 4)
                    if i + 3 < NS * 4:
                        load_s(i + 3)
                    if i + 1 < NS * 4:
                        bc_v(i + 1)
                    si = sin_[i % 4]
                    k = i % 2
                    OP("scalar", "mul", [("sin", i % 4)], [("sin", i % 4)], out=si[:], in_=si[:], mul=float(G_RET[h]))
                    for c in range(2):
                        OP("vector", "scalar_tensor_tensor", [("pvb", k), "kTfs", ("sin", i % 4)], [("snew", k)],
                           out=snew[k][:, c, :], in0=pvb[k][:, :], scalar=kTf[:, 2 * h + c, b:b + 1],
                           in1=si[:, c, :], op0=ALU.mult, op1=ALU.add)
                    DMA(ret_s[b, h].rearrange("(c p) v -> p c v", c=2), snew[k][:], [("snew", k)], [("o_ret_s", i)])
                    OP("scalar", "copy", [("snew", k)], [("snb", k)], out=snb[k][:], in_=snew[k][:])
                    MM([("snb", k), "qmask"], [("pos", h)],
                       [dict(out=pos[h][:NS, :], lhsT=qmask[:, 2 * h + c, b, :], rhs=snb[k][:, c, :],
                             start=(b == 0 and c == 0), stop=(b == NS - 1 and c == 1)) for c in range(2)])
                for h in range(4):
                    head_norm_gate(pos[h][:NS, :], ("pos", h), NS, sg, "sgs", ya, "yas", h, ssh[:, h:h + 1], ("sshs", h))
                for half in range(2):
                    TR(["yas", "identb"], ["pq"],
                       [dict(out=pT[:, c, :NS], in_=ya[:, (half * 8 + c) * 128:(half * 8 + c + 1) * 128],
                             identity=identb[:NS, :NS]) for c in range(8)])
                    OP("scalar", "copy", ["pq"], ["yTs"], out=yT[:, half * 8:(half + 1) * 8, :], in_=pT[:, :, :NS])
                DMAS(yaT_all[NT, :, :, 0:NS], yT[:], ["yTs"], [("yaT_all", NT)])
                S.flush()
            S.barrier()


        C0 = -float(np.exp(-0.5))
        if upto >= "B":
          with contextlib.ExitStack() as stB:
            sbB, psB = mk(stB)
            wB = sbB("wB", [128, 8, 3200], BF16)
            mubc = sbB("mubc", [128, 3200])
            kkbc = sbB("kkbc", [128, 1024])
            kabc = sbB("kabc", [128, 1024])
            rkbc = sbB("rkbc", [128, 1024])
            w2x = sbB("w2x", [65, 1024], BF16)
            a2x = sbB("a2x", [65, 1024], BF16)
            tri = sbB("tri_sb", [128, 2, 128])
            c0col = sbB("c0col_sb", [128, 1])
            DMA(mubc[:], rw_mu.partition_broadcast(128), [], ["mubc"])
            DMA(kkbc[:], rw_k_k.partition_broadcast(128), [], ["kkbc"])
            DMA(kabc[:], rw_k_a.partition_broadcast(128), [], ["kabc"])
            DMA(rkbc[:], rw_r_k.partition_broadcast(128), [], ["rkbc"])
            DMA(tri[:, 0, :], tri_d[0], [], ["tri"])
            DMA(tri[:, 1, :], tri_d[1], [], ["tri"])
            DMA(c0col[:], c0col_d[:, :], [], ["c0col"])
            with contextlib.ExitStack() as st:
                sb, ps = mk(st)
                stg = [sb("stgB%d" % i, [128, 2048]) for i in range(6)]
                load_w(wB, w_in[:, O4:O5], D, 3200, stg, "wB")
                DMA(stg[0][0:64, 0:1024], rw_w2[:, :], [("stg", 0)], [("stg", 0)])
                DMA(stg[0][64:65, 0:1024], rw_w0[:, :], [("stg", 0)], [("stg", 0)])
                OP("vector", "tensor_copy", [("stg", 0)], ["w2x"], out=w2x[:], in_=stg[0][0:65, 0:1024])
                DMA(stg[1][0:64, 0:1024], rw_a2[:, :], [("stg", 1)], [("stg", 1)])
                DMA(stg[1][64:65, 0:1024], rw_a0[:, :], [("stg", 1)], [("stg", 1)])
                OP("vector", "tensor_copy", [("stg", 1)], ["a2x"], out=a2x[:], in_=stg[1][0:65, 0:1024])
            S.barrier()

            with contextlib.ExitStack() as st:
                sb, ps = mk(st)
                hTt = [sb("hTtB%d" % i, [128, 8, 128], BF16) for i in range(3)]
                ssTt = sb("ssTt", [128, 8, NS], BF16)
                sh = sb("sh", [128, 3200])
                zs = [sb("z%d" % i, [128, 3200]) for i in range(2)]
                lo = sb("lo", [128, 128], BF16)
                loT = sb("loT", [65, 2, 128], BF16)
                sgw = sb("sgw", [128, 1024])
                av = sb("av", [128, 1024])
                kk = sb("kk", [128, 1024])
                k2 = sb("k2", [128, 1024])
                bb = sb("bb", [128, 1024])
                tmpf = sb("tmpf", [128, 1024])
                e1 = sb("e1", [128, 1024])
                e2 = sb("e2", [128, 1024])
                e3 = sb("e3", [128, 1024])
                rt = sb("rt", [128, 1024], BF16)
                kt = sb("kt", [128, 1024], BF16)
                small = sb("smallB", [128, 64])
                packB = [sb("packB%d" % i, [128, 7168], BF16) for i in range(2)]
                packF = [sb("packF%d" % i, [128, 1032]) for i in range(2)]
                pg = [ps("pgB%d" % i, [128, 1024]) for i in range(3)]
                pTf = ps("pTB", [128, 512])
                pT = pTf[:, :].bitcast(BF16).rearrange("p (c f) -> p c f", c=8)
                pgc = ps("pgc", [128, 512])
                OP("vector", "memset", [], ["loT"], ap=loT[64:65, :, :], constant=1.0)

                pgi = [0]

                def nextpg():
                    k = pgi[0] % 3
                    pgi[0] += 1
                    return pg[k], ("pgB", k)

                def projB(hT_, hk, rows, dst, dstk):
                    for g in range(4):
                        p_, pk_ = nextpg()
                        for j in range(2):
                            c0_ = g * 1024 + j * 512
                            if c0_ >= 3200:
                                continue
                            c1_ = min(3200, c0_ + 512)
                            MM([hk, "wB"], [pk_],
                               [dict(out=p_[:rows, j * 512:j * 512 + (c1_ - c0_)], lhsT=hT_[:, kc, :rows],
                                     rhs=wB[:, kc, c0_:c1_], start=(kc == 0), stop=(kc == 7)) for kc in range(8)])
                        w_ = min(3200, (g + 1) * 1024) - g * 1024
                        OP("scalar", "copy", [pk_], [dstk], out=dst[:rows, g * 1024:g * 1024 + w_], in_=p_[:rows, :w_])

                def preB_s1(t):
                    b = t % 2
                    rows = 128 if t < NT else NS
                    R = slice(0, rows)
                    z = zs[b]
                    zk_ = ("z", b)
                    projB(hTt[t % 3], ("hTtB", t % 3), rows, sh, "sh")
                    if t < NT:
                        fns = [lambda e, z=z: e.dma_start(out=z[1:113, :], in_=sh[0:112, :]),
                               lambda e, z=z: e.dma_start(out=z[113:128, :], in_=sh[112:127, :])]
                        if t > 0:
                            fns.append(lambda e, z=z: e.dma_start(out=z[0:1, :], in_=lastrow_d[:, :]))
                        S.dma_multi(fns, ["sh", "lastrow"], [zk_])
                        if t == 0:
                            OP("vector", "memset", [], [zk_], ap=z[0:1, :], constant=0.0)
                        DMA(lastrow_d[:, :], sh[127:128, :], ["sh"], ["lastrow"])
                    else:
                        projB(ssTt, "ssTt", rows, z, zk_)
                    OP("vector", "tensor_tensor", [zk_, "sh"], [zk_], out=z[R, :], in0=z[R, :], in1=sh[R, :], op=ALU.subtract)
                    OP("vector", "tensor_tensor", [zk_, "mubc"], [zk_], out=z[R, :], in0=z[R, :], in1=mubc[R, :], op=ALU.mult)
                    OP("vector", "tensor_tensor", [zk_, "sh"], [zk_], out=z[R, :], in0=z[R, :], in1=sh[R, :], op=ALU.add)

                def preB_s2(t):
                    b = t % 2
                    rows = 128 if t < NT else NS
                    R = slice(0, rows)
                    pb_, pf_ = packB[b], packF[b]
                    z = zs[b]
                    zk_ = ("z", b)
                    zr, zk, zv = z[R, 0:1024], z[R, 1024:2048], z[R, 2048:3072]
                    OP("scalar", "activation", [zk_], ["lo"], out=lo[R, 0:64], in_=z[R, 3072:3136], func=AF.Tanh)
                    OP("vector", "tensor_copy", [zk_], ["lo"], out=lo[R, 64:128], in_=z[R, 3136:3200])
                    TR(["lo", "identb"], ["pTB"],
                       [dict(out=pT[0:64, j, :rows], in_=lo[R, j * 64:(j + 1) * 64], identity=identb[R, R]) for j in range(2)])
                    OP("scalar", "copy", ["pTB"], ["loT"], out=loT[0:64, :, :rows], in_=pT[0:64, 0:2, :rows])
                    pw_, pwk = nextpg()
                    MM(["loT", "w2x"], [pwk],
                       [dict(out=pw_[R, j * 512:(j + 1) * 512], lhsT=loT[0:65, 0, :rows], rhs=w2x[0:65, j * 512:(j + 1) * 512],
                             start=True, stop=True) for j in range(2)])
                    OP("scalar", "activation", [pwk], ["sgw"], out=sgw[R, :], in_=pw_[R, :], func=AF.Sigmoid)
                    pa_, pak = nextpg()
                    MM(["loT", "a2x"], [pak],
                       [dict(out=pa_[R, j * 512:(j + 1) * 512], lhsT=loT[0:65, 1, :rows], rhs=a2x[0:65, j * 512:(j + 1) * 512],
                             start=True, stop=True) for j in range(2)])
                    OP("scalar", "activation", [pak], ["av"], out=av[R, :], in_=pa_[R, :], func=AF.Sigmoid)
                    OP("gpsimd", "tensor_tensor", [zk_, "kkbc"], ["kk"], out=kk[R, :], in0=zk, in1=kkbc[R, :], op=ALU.mult)
                    OP("scalar", "activation", ["kk"], ["tmpf"], out=tmpf[R, :], in_=kk[R, :], func=AF.Square)
                    OP("vector", "tensor_reduce", ["tmpf"], ["smallB"], out=small[R, 0:16],
                       in_=tmpf[R, :].rearrange("p (h f) -> p h f", h=16), axis=AX.X, op=ALU.add)
                    OP("vector", "tensor_scalar_max", ["smallB"], ["smallB"], out=small[R, 0:16], in0=small[R, 0:16], scalar1=1e-24)
                    OP("gpsimd", "tensor_tensor", ["smallB", "nhalf"], ["smallB"], out=small[R, 0:16], in0=small[R, 0:16],
                       in1=nhalf[R, 0:16], op=ALU.pow)
                    OP("vector", "tensor_tensor", ["kk", "smallB"], ["kk"], out=kk[R, :].rearrange("p (h f) -> p h f", h=16),
                       in0=kk[R, :].rearrange("p (h f) -> p h f", h=16),
                       in1=small[R, 0:16].unsqueeze(2).to_broadcast([rows, 16, 64]), op=ALU.mult)
                    OP("vector", "scalar_tensor_tensor", ["av", "kabc"], ["tmpf"], out=tmpf[R, :], in0=av[R, :], scalar=-1.0,
                       in1=kabc[R, :], op0=ALU.add, op1=ALU.mult)
                    OP("vector", "scalar_tensor_tensor", ["tmpf", zk_], ["k2"], out=k2[R, :], in0=tmpf[R, :], scalar=1.0,
                       in1=zk, op0=ALU.add, op1=ALU.mult)
                    OP("gpsimd", "tensor_tensor", ["av", "kk"], ["bb"], out=bb[R, :], in0=av[R, :], in1=kk[R, :], op=ALU.mult)
                    OP("gpsimd", "tensor_tensor", [zk_, "k2"], ["tmpf"], out=tmpf[R, :], in0=zr, in1=k2[R, :], op=ALU.mult)
                    OP("gpsimd", "tensor_tensor", ["tmpf", "rkbc"], ["tmpf"], out=tmpf[R, :], in0=tmpf[R, :], in1=rkbc[R, :], op=ALU.mult)
                    OP("vector", "tensor_reduce", ["tmpf"], ["smallB2"], out=small[R, 16:32],
                       in_=tmpf[R, :].rearrange("p (h f) -> p h f", h=16), axis=AX.X, op=ALU.add)
                    if t < NT:
                        OP("gpsimd", "tensor_tensor", [zk_, "smallB2"], [("packF", b)],
                           out=pf_[R, 0:1024].rearrange("p (h f) -> p h f", h=16),
                           in0=zv.rearrange("p (h f) -> p h f", h=16),
                           in1=small[R, 16:32].unsqueeze(2).to_broadcast([rows, 16, 64]), op=ALU.mult)
                        OP("scalar", "copy", [zk_], [("packB", b)], out=pb_[R, 6144:7168], in_=zv)
                        pc_, pck = nextpg()
                        MM(["tri", "sgw"], [pck],
                           [dict(out=pc_[:, j * 512:(j + 1) * 512], lhsT=tri[:, 0, :], rhs=sgw[:, j * 512:(j + 1) * 512],
                                 start=True, stop=True) for j in range(2)])
                        OP("scalar", "activation", [pck], ["e1"], out=e1[:], in_=pc_[:, :], func=AF.Exp)
                        OP("scalar", "activation", [pck], ["e2"], out=e2[:], in_=pc_[:, :], func=AF.Exp, scale=-1.0)
                        ps_, psk = nextpg()
                        MM(["tri", "sgw"], [psk],
                           [dict(out=ps_[:, j * 512:(j + 1) * 512], lhsT=tri[:, 1, :], rhs=sgw[:, j * 512:(j + 1) * 512],
                                 start=True, stop=True) for j in range(2)])
                        OP("scalar", "activation", [psk], ["e3"], out=e3[:], in_=ps_[:, :], func=AF.Exp)
                        MM(["sgw", "c0col"], ["pgc"],
                           [dict(out=pgc[:, hh:hh + 1], lhsT=sgw[:, hh * 128:(hh + 1) * 128], rhs=c0col[:, 0:1],
                                 start=True, stop=True) for hh in range(8)])
                        OP("scalar", "activation", ["pgc"], [("packF", b)], out=pf_[:, 1024:1032], in_=pgc[:, 0:8], func=AF.Exp)
                        OP("vector", "tensor_tensor", [zk_, "e1"], ["rt"], out=rt[:], in0=zr, in1=e1[:], op=ALU.mult)
                        OP("gpsimd", "tensor_tensor", ["k2", "e2"], [("packB", b)], out=pb_[:, 5120:6144], in0=k2[:], in1=e2[:], op=ALU.mult)
                        OP("vector", "tensor_tensor", ["bb", "e2"], [("packB", b)], out=pb_[:, 4096:5120], in0=bb[:], in1=e2[:], op=ALU.mult)
                        OP("gpsimd", "tensor_tensor", ["kk", "e3"], ["kt"], out=kt[:], in0=kk[:], in1=e3[:], op=ALU.mult)
                        RKv = pb_[:, 0:2048].rearrange("p (c j f) -> p c j f", c=8, j=2)
                        for (src, srck, dst) in ((kt, "kt", RKv[:, :, 0, :]), (rt, "rt", RKv[:, :, 1, :]),
                                                 (pb_[:, 4096:5120], ("packB", b), pb_[:, 2048:3072].rearrange("p (c f) -> p c f", c=8)),
                                                 (pb_[:, 5120:6144], ("packB", b), pb_[:, 3072:4096].rearrange("p (c f) -> p c f", c=8))):
                            TR([srck, "identb"], ["pTB"],
                               [dict(out=pT[:, c, :], in_=src[:, c * 128:(c + 1) * 128], identity=identb[:, :]) for c in range(8)])
                            OP("scalar", "copy", ["pTB"], [("packB", b)], out=dst, in_=pT)
                        DMA(preB[t], pb_[:], [("packB", b)], [("preB", t)])
                        DMA(preF[t], pf_[:], [("packF", b)], [("preF", t)])
                    else:
                        OP("gpsimd", "tensor_tensor", [zk_, "smallB2"], [("packF", b)],
                           out=pf_[R, 0:1024].rearrange("p (h f) -> p h f", h=16),
                           in0=zv.rearrange("p (h f) -> p h f", h=16),
                           in1=small[R, 16:32].unsqueeze(2).to_broadcast([rows, 16, 64]), op=ALU.mult)
                        DMA(preF[t, 0:NS, 0:1024], pf_[R, 0:1024], [("packF", b)], [("preF", t)])
                        OP("scalar", "activation", ["sgw"], ["e1"], out=e1[R, :], in_=sgw[R, :], func=AF.Exp, scale=C0)
                        DMA(sampre[:, 0, :], kk[R, :], ["kk"], [("sampre", 0)])
                        DMA(sampre[:, 1, :], e1[R, :], ["e1"], [("sampre", 1)])
                        DMA(sampre[:, 2, :], bb[R, :], ["bb"], [("sampre", 2)])
                        DMA(sampre[:, 3, :], k2[R, :], ["k2"], [("sampre", 3)])
                        DMA(sampre[:, 4, :], zr, [zk_], [("sampre", 4)])
                        DMA(sampre[:, 5, :], zv, [zk_], [("sampre", 5)])

                def loadB(t):
                    b = t % 3
                    if t < NT:
                        DMA(hTt[b][:], hT_all[:, :, t * 128:(t + 1) * 128], [("hT_all", t)], [("hTtB", b)])
                    else:
                        DMA(hTt[b][:, :, 0:NS], hT_all[:, :, T:T + NS], [("hT_all", NT)], [("hTtB", b)])
                        DMA(ssTt[:], ssT_d[:, :, :], [("hT_all", NT + 1)], ["ssTt"])

                S.record()
                loadB(0)
                loadB(1)
                preB_s1(0)
                for t in range(NT + 1):
                    if t + 2 <= NT:
                        loadB(t + 2)
                    if t + 1 <= NT:
                        preB_s1(t + 1)
                    preB_s2(t)
                S.flush()
            S.barrier()

        if upto >= "B":
          with contextlib.ExitStack() as st:
            sb, ps = mk(st)
            lnwbc = sb("lnwbc", [128, 1024])
            lnbbc = sb("lnbbc", [128, 1024])
            mask3 = sb("mask3_sb", [128, 3, 128])
            nmSU = sb("nmSU", [128, 128])
            nmSL = sb("nmSL", [128, 128])
            DMA(lnwbc[:], rw_ln_w.partition_broadcast(128), [], ["lnwbc"])
            DMA(lnbbc[:], rw_ln_b.partition_broadcast(128), [], ["lnbbc"])
            DMA(mask3[:], mask3_d[:, :, :], [], ["mask3"])
            DMA(nmSU[:], masks_d[1], [], ["nmSU"])
            DMA(nmSL[:], masks_d[2], [], ["nmSL"])
            OP("vector", "tensor_scalar_mul", ["nmSU"], ["nmSU"], out=nmSU[:], in0=nmSU[:], scalar1=-1.0)
            OP("vector", "tensor_scalar_mul", ["nmSL"], ["nmSL"], out=nmSL[:], in0=nmSL[:], scalar1=-1.0)
            osb = sb("osb", [128, 1024])
            oc = sb("oc", [128, 1024])
            sq = sb("sqR", [128, 1024])
            yb = sb("ybR", [128, 1024])
            small = sb("smallR", [128, 64])
            stP = contextlib.ExitStack()
            sbP, _psP = mk(stP)
            pg = [_psP("pgR%d" % i, [128, 1024]) for i in range(1)]
            pr = [_psP("prR%d" % i, [128, 512]) for i in range(6)]
            pkB = [sbP("pkB%d" % i, [128, 7168], BF16) for i in range(3)]
            pkF = [sbP("pkF%d" % i, [128, 1032]) for i in range(3)]
            G3s = [sbP("G3_%d" % i, [128, 16, 3, 128], BF16) for i in range(2)]
            PXs = [[sbP("PX%d_%d" % (j, i), [128, 16, 3, 128], BF16) for i in range(2)] for j in range(2)]
            Wn = sbP("Wn", [128, 1024], BF16)
            U = sbP("U", [128, 1024], BF16)
            Hst = sbP("Hst", [128, 8, 64])
            Hbf = sbP("Hbf", [128, 8, 64], BF16)
            HT = sbP("HT", [64, 8, 128])
            OP("vector", "memset", [], ["Hst"], ap=Hst[:], constant=0.0)
            OP("gpsimd", "memset", [], ["Hbf"], ap=Hbf[:], constant=0.0)
            pri = [0]

            def nextpr():
                k = pri[0] % 6
                pri[0] += 1
                return pr[k], ("prR", k)

            def loadR(t):
                b = t % 3
                DMA(pkB[b][:], preB[t], [("preB", t)], [("pkB", b)])
                DMA(pkF[b][:], preF[t], [("preF", t)], [("pkF", b)])

            def gn_post(rows, bonus, bonusk, t):
                R = slice(0, rows)
                v16 = lambda ap: ap.rearrange("p (h f) -> p h f", h=16)
                OP("vector", "tensor_reduce", ["osb"], ["smR0"], out=small[R, 0:16], in_=v16(osb[R, :]), axis=AX.X, op=ALU.add)
                OP("vector", "scalar_tensor_tensor", ["smR0", "osb"], ["oc"], out=v16(oc[R, :]),
                   in0=small[R, 0:16].unsqueeze(2).to_broadcast([rows, 16, 64]), scalar=-1.0 / 64,
                   in1=v16(osb[R, :]), op0=ALU.mult, op1=ALU.add)
                OP("gpsimd", "tensor_tensor", ["oc"], ["sqR"], out=sq[R, :], in0=oc[R, :], in1=oc[R, :], op=ALU.mult)
                OP("vector", "tensor_reduce", ["sqR"], ["smR1"], out=small[R, 16:32], in_=v16(sq[R, :]), axis=AX.X, op=ALU.add)
                rsqrt_chain(small[R, 16:32], "smR1", rows, 1.0 / 64, GN_EPS)
                OP("vector", "tensor_tensor", ["oc", "smR1"], ["oc"], out=v16(oc[R, :]), in0=v16(oc[R, :]),
                   in1=small[R, 16:32].unsqueeze(2).to_broadcast([rows, 16, 64]), op=ALU.mult)
                OP("gpsimd", "tensor_tensor", ["oc", "lnwbc"], ["oc"], out=oc[R, :], in0=oc[R, :], in1=lnwbc[R, :], op=ALU.mult)
                OP("gpsimd", "tensor_tensor", ["oc", "lnbbc"], ["oc"], out=oc[R, :], in0=oc[R, :], in1=lnbbc[R, :], op=ALU.add)
                OP("gpsimd", "tensor_tensor", ["oc", bonusk], ["ybR"], out=yb[R, :], in0=oc[R, :], in1=bonus, op=ALU.add)
                DMAS(yb_all[t, 0:rows, :], yb[R, :], ["ybR"], [("yb_all", t)])

            def rec_views(t):
                b = t % 3
                pb_, pf_ = pkB[b], pkF[b]
                pbk, pfk = ("pkB", b), ("pkF", b)
                RK = pb_[:, 0:2048].rearrange("p (c j f) -> p c j f", c=8, j=2)
                bT = pb_[:, 2048:3072].rearrange("p (c f) -> p c f", c=8)
                kT = pb_[:, 3072:4096].rearrange("p (c f) -> p c f", c=8)
                btok = pb_[:, 4096:5120]
                ktok = pb_[:, 5120:6144]
                vbf = pb_[:, 6144:7168]
                return pb_, pf_, pbk, pfk, RK, bT, kT, btok, ktok, vbf

            def rec_s1(t):
                q = t % 2
                PX = PXs[q]
                G3 = G3s[q]
                pb_, pf_, pbk, pfk, RK, bT, kT, btok, ktok, vbf = rec_views(t)
                P0 = PX[0]
                for hg in range(4):
                    pn_, pnk = nextpr()
                    for j in range(4):
                        h = hg * 4 + j
                        hh, hp = h // 2, h % 2
                        Pp = slice(hp * 64, hp * 64 + 64)
                        p_, pk_ = nextpr()
                        MM([pbk], [pk_],
                           [dict(out=p_[:, 0:256], lhsT=bT[Pp, hh, :], rhs=RK[Pp, hh, :, :], start=True, stop=True),
                            dict(out=p_[:, 256:512], lhsT=kT[Pp, hh, :], rhs=RK[Pp, hh, :, :], start=True, stop=True)])
                        OP("vector", "tensor_tensor", [pk_, "nmSU"], [("PXP", q, 0, h)], out=P0[:, h, 1, :], in0=p_[:, 0:128],
                           in1=nmSU[:, :], op=ALU.mult)
                        OP("vector", "tensor_tensor", [pk_, "mask3"], [("G3", q, h)], out=G3[:, h, :, :],
                           in0=p_[:, 128:512].rearrange("p (j f) -> p j f", j=3), in1=mask3[:, :, :], op=ALU.mult)
                        MM([pbk], [pnk],
                           [dict(out=pn_[:, j * 128:(j + 1) * 128], lhsT=RK[Pp, hh, 0, :], rhs=bT[Pp, hh, :], start=True, stop=True)])
                    OP("vector", "tensor_tensor", [pnk, "nmSL"], [("PXT", q, 0, hg)], out=P0[:, hg * 4:(hg + 1) * 4, 2, :],
                       in0=pn_[:, :].rearrange("p (j f) -> p j f", j=4),
                       in1=nmSL[:, :].unsqueeze(1).to_broadcast([128, 4, 128]), op=ALU.mult)
                OP("gpsimd", "memset", [], [("PXX", q, 0, h) for h in range(16)], ap=P0[:, :, 0, :], constant=0.0)
                OP("gpsimd", "tensor_tensor", [("PXX", q, 0, h) for h in range(16)] + ["identb"],
                   [("PXX", q, 0, h) for h in range(16)],
                   out=P0[:, :, 0, :], in0=P0[:, :, 0, :],
                   in1=identb[:, :].unsqueeze(1).to_broadcast([128, 16, 128]), op=ALU.add)
                for r in range(7):
                    cur, nxt = PX[r % 2], PX[(r + 1) % 2]
                    ci, ni = r % 2, (r + 1) % 2
                    for g in range(4):
                        hs = list(range(4 * g, 4 * g + 4))
                        rd = [("PXP", q, ci, h) for h in hs] + [("PXX", q, ci, h) for h in hs] + [("PXT", q, ci, g)]
                        xb, xbk = nextpr()
                        MM(rd, [xbk],
                           [dict(out=xb[:, j * 128:(j + 1) * 128], lhsT=cur[:, h, 2, :], rhs=cur[:, h, 0, :], start=True, stop=True)
                            for j, h in enumerate(hs)])
                        if r < 6:
                            for half in range(2):
                                pb2, pb2k = nextpr()
                                h0 = 4 * g + 2 * half
                                mms = []
                                for j in range(2):
                                    h = h0 + j
                                    mms.append(dict(out=pb2[:, j * 256:j * 256 + 128], lhsT=cur[:, h, 2, :], rhs=cur[:, h, 1, :],
                                                    start=True, stop=True))
                                    mms.append(dict(out=pb2[:, j * 256 + 128:(j + 1) * 256], lhsT=cur[:, h, 1, :], rhs=cur[:, h, 2, :],
                                                    start=True, stop=True))
                                MM(rd, [pb2k], mms)
                                OP("scalar", "copy", [pb2k], [("PXP", q, ni, h0), ("PXP", q, ni, h0 + 1), ("PXT", q, ni, g)],
                                   out=nxt[:, h0:h0 + 2, 1:3, :],
                                   in_=pb2[:, :].rearrange("p (a j f) -> p a j f", a=2, j=2))
                        OP("vector", "tensor_tensor", [xbk] + [("PXX", q, ci, h) for h in hs], [("PXX", q, ni, h) for h in hs],
                           out=nxt[:, 4 * g:4 * g + 4, 0, :], in0=xb[:, :].rearrange("p (j f) -> p j f", j=4),
                           in1=cur[:, 4 * g:4 * g + 4, 0, :], op=ALU.add)

            def rec_s2(t):
                q = t % 2
                PX = PXs[q]
                G3 = G3s[q]
                pb_, pf_, pbk, pfk, RK, bT, kT, btok, ktok, vbf = rec_views(t)
                pW, pWk = pg[0], ("pgR", 0)
                mms = []
                for h in range(16):
                    hh, hp = h // 2, h % 2
                    Pp = slice(hp * 64, hp * 64 + 64)
                    mms.append(dict(out=pW[:, h * 64:(h + 1) * 64], lhsT=RK[Pp, hh, 0, :], rhs=Hbf[Pp, hh, :], start=True, stop=False))
                    mms.append(dict(out=pW[:, h * 64:(h + 1) * 64], lhsT=G3[:, h, 1, :], rhs=vbf[:, h * 64:(h + 1) * 64], start=False, stop=True))
                MM([pbk, "Hbf"] + [("G3", q, h) for h in range(16)], [pWk], mms)
                OP("scalar", "mul", [pWk], ["Wn"], out=Wn[:], in_=pW[:, :], mul=-1.0)
                Xf = PX[1]
                xk = [("PXX", q, 1, h) for h in range(16)]
                pU, pUk = pg[0], ("pgR", 0)
                MM(xk + ["Wn"], [pUk],
                   [dict(out=pU[:, h * 64:(h + 1) * 64], lhsT=Xf[:, h, 0, :], rhs=Wn[:, h * 64:(h + 1) * 64], start=True, stop=True)
                    for h in range(16)])
                OP("scalar", "copy", [pUk], ["U"], out=U[:], in_=pU[:, :])
                pO, pOk = pg[0], ("pgR", 0)
                mms = []
                for h in range(16):
                    hh, hp = h // 2, h % 2
                    Pp = slice(hp * 64, hp * 64 + 64)
                    o_ = pO[:, h * 64:(h + 1) * 64]
                    mms.append(dict(out=o_, lhsT=RK[Pp, hh, 1, :], rhs=Hbf[Pp, hh, :], start=True, stop=False))
                    mms.append(dict(out=o_, lhsT=G3[:, h, 0, :], rhs=U[:, h * 64:(h + 1) * 64], start=False, stop=False))
                    mms.append(dict(out=o_, lhsT=G3[:, h, 2, :], rhs=vbf[:, h * 64:(h + 1) * 64], start=False, stop=True))
                MM([pbk, "Hbf", "U"] + [("G3", q, h) for h in range(16)], [pOk], mms)
                OP("scalar", "copy", [pOk], ["osb"], out=osb[:, :], in_=pO[:, :])
                gn_post(128, pf_[:, 0:1024], pfk, t)
                pH, pHk = pg[0], ("pgR", 0)
                pHv = pH[:, :].rearrange("p (c f) -> p c f", c=8)
                mms = []
                for hh in range(8):
                    mms.append(dict(out=pHv[:, hh, :], lhsT=btok[:, hh * 128:(hh + 1) * 128], rhs=U[:, hh * 128:(hh + 1) * 128], start=True, stop=False))
                    mms.append(dict(out=pHv[:, hh, :], lhsT=ktok[:, hh * 128:(hh + 1) * 128], rhs=vbf[:, hh * 128:(hh + 1) * 128], start=False, stop=True))
                MM([pbk, "U"], [pHk], mms)
                for hp in range(2):
                    Pp = slice(hp * 64, hp * 64 + 64)
                    OP("vector", "tensor_tensor", [pHk, "Hst"], ["Hst"], out=Hst[Pp, :, :], in0=Hst[Pp, :, :],
                       in1=pHv[Pp, :, hp * 64:(hp + 1) * 64], op=ALU.add)
                OP("vector", "tensor_tensor", ["Hst", pfk], ["Hst"], out=Hst[:], in0=Hst[:],
                   in1=pf_[:, 1024:1032].unsqueeze(2).to_broadcast([128, 8, 64]), op=ALU.mult)
                OP("gpsimd", "tensor_copy", ["Hst"], ["Hbf"], out=Hbf[:], in_=Hst[:])

            S.record()
            loadR(0)
            loadR(1)
            rec_s1(0)
            for t in range(NT):
                if t + 2 < NT:
                    loadR(t + 2)
                if t + 1 < NT:
                    rec_s1(t + 1)
                rec_s2(t)
            pHT = pg[0][0:64, :].rearrange("p (c f) -> p c f", c=8)
            TR(["Hst", "identf"], [("pgR", 0)],
               [dict(out=pHT[:, hh, :], in_=Hst[:, hh, :], identity=identf[:, :]) for hh in range(8)])
            OP("vector", "tensor_copy", [("pgR", 0)], ["HT"], out=HT[:], in_=pHT)
            DMA(rwkv_p.rearrange("(hh hp) v c -> v hh hp c", hp=2), HT[:].rearrange("v hh (hp c) -> v hh hp c", hp=2),
                ["HT"], ["o_rwkv_p"])
            S.flush()
            S.barrier()
            stP.close()
            S.record()

            pg = [ps("pgS%d" % i, [128, 1024]) for i in range(2)]
            pr = [ps("prS%d" % i, [128, 512]) for i in range(2)]
            samt = sb("samt", [NS, 6, 1024])
            bon_s = sb("bon_s", [NS, 1024])
            sel2 = sb("sel2_sb", [NS, 8, 128])
            hm = sb("hm_sb", [NS, 128])
            Em = sb("Em_sb", [NS, 8])
            Fm = sb("Fm_sb", [128, 64])
            hmask = sb("hmask_sb", [128, 2])
            vdm = sb("vdm", [NS, 16, 128])
            vfm = sb("vfm", [128, 16, 8])
            ofm = sb("ofm", [128, 16, 8])
            ofmm = sb("ofmm", [128, 16, 2, 8])
            St = [sb("St%d" % i, [128, 16, 64]) for i in range(2)]
            Sn = [sb("Sn%d" % i, [128, 16, 64]) for i in range(2)]
            tA = sb("tA", [128, 16, 64])
            tB = sb("tB", [128, 16, 64])
            sk = sb("sk_s", [128, 16])
            DMA(samt[:], sampre[:, 0:6, :], [("sampre", i) for i in range(6)], ["samt"])
            DMA(bon_s[:], preF[NT, 0:NS, 0:1024], [("preF", NT)], ["bon_s"])
            DMA(sel2[:], sel2_d[:, :, :], [], ["sel2"])
            DMA(hm[:], hm_d[:, :], [], ["hm"])
            DMA(Em[:], Em_d[:, :], [], ["Em"])
            DMA(Fm[:], Fm_d[:, :], [], ["Fm"])
            DMA(hmask[:], hmask_d[:, :], [], ["hmask"])
            OP("gpsimd", "tensor_tensor", ["samt", "hm"], ["vdm"], out=vdm[:].rearrange("p h (j f) -> p h j f", j=2),
               in0=samt[:, 5, :].rearrange("p (h f) -> p h f", h=16).unsqueeze(2).to_broadcast([NS, 16, 2, 64]),
               in1=hm[:, :].rearrange("p (j f) -> p j f", j=2).unsqueeze(1).to_broadcast([NS, 16, 2, 64]), op=ALU.mult)
            MM(["vdm", "Em"], [("prR", 0)],
               [dict(out=pr[0][:, h * 8:(h + 1) * 8], lhsT=vdm[:, h, :], rhs=Em[:, :], start=True, stop=True) for h in range(16)])
            OP("vector", "tensor_copy", [("prR", 0)], ["vfm"], out=vfm[:], in_=pr[0][:, 0:128].rearrange("p (h f) -> p h f", h=16))

            def load_st(bbi):
                i = bbi % 2
                DMA(St[i][0:64, :, :], srw[bbi].rearrange("h v k -> v h k"), [], [("St", i)])
                DMA(St[i][64:128, :, :], srw[8 + bbi].rearrange("h v k -> v h k"), [], [("St", i)])

            pgi = [0]

            def bcast(q, bbi):
                k = pgi[0] % 2
                pgi[0] += 1
                MM(["sel2", "samt"], [("pgS", k)],
                   [dict(out=pg[k][:, j * 512:(j + 1) * 512], lhsT=sel2[:, bbi, :], rhs=samt[:, q, j * 512:(j + 1) * 512],
                         start=True, stop=True) for j in range(2)])
                return pg[k][:, :].rearrange("p (h f) -> p h f", h=16), ("pgS", k)

            load_st(0)
            for bbi in range(8):
                i = bbi % 2
                if bbi + 1 < 8:
                    load_st(bbi + 1)
                stk, snk = ("St", i), ("Sn", i)
                pk_, pkk = bcast(0, bbi)
                OP("vector", "tensor_tensor", [stk, pkk], ["tA"], out=tA[:], in0=St[i][:], in1=pk_, op=ALU.mult)
                OP("vector", "tensor_reduce", ["tA"], ["sk_s"], out=sk[:], in_=tA[:], axis=AX.X, op=ALU.add)
                pw_, pwk = bcast(1, bbi)
                OP("vector", "tensor_tensor", [stk, pwk], [snk], out=Sn[i][:], in0=St[i][:], in1=pw_, op=ALU.mult)
                pb_, pbk = bcast(2, bbi)
                OP("vector", "tensor_tensor", [pbk, "sk_s"], ["tB"], out=tB[:], in0=pb_,
                   in1=sk[:, :].unsqueeze(2).to_broadcast([128, 16, 64]), op=ALU.mult)
                OP("vector", "tensor_tensor", [snk, "tB"], [snk], out=Sn[i][:], in0=Sn[i][:], in1=tB[:], op=ALU.subtract)
                pk2_, pk2k = bcast(3, bbi)
                OP("vector", "tensor_tensor", [pk2k, "vfm"], ["tA"], out=tA[:], in0=pk2_,
                   in1=vfm[:, :, bbi:bbi + 1].to_broadcast([128, 16, 64]), op=ALU.mult)
                OP("vector", "tensor_tensor", [snk, "tA"], [snk], out=Sn[i][:], in0=Sn[i][:], in1=tA[:], op=ALU.add)
                DMAS(rwkv_s[bbi].rearrange("h v k -> v h k"), Sn[i][0:64, :, :], [snk], [("o_rwkv_s", bbi)])
                DMAS(rwkv_s[8 + bbi].rearrange("h v k -> v h k"), Sn[i][64:128, :, :], [snk], [("o_rwkv_s", 8 + bbi)])
                pr_, prk = bcast(4, bbi)
                OP("vector", "tensor_tensor", [snk, prk], ["tB"], out=tB[:], in0=Sn[i][:], in1=pr_, op=ALU.mult)
                OP("vector", "tensor_reduce", ["tB"], ["ofm"], out=ofm[:, :, bbi], in_=tB[:], axis=AX.X, op=ALU.add)
            OP("vector", "tensor_tensor", ["ofm", "hmask"], ["ofmm"], out=ofmm[:],
               in0=ofm[:].unsqueeze(2).to_broadcast([128, 16, 2, 8]),
               in1=hmask[:, :].unsqueeze(1).unsqueeze(3).to_broadcast([128, 16, 2, 8]), op=ALU.mult)
            for half in range(2):
                MM(["ofmm", "Fm"], [("prR", half)],
                   [dict(out=pr[half][0:NS, j * 64:(j + 1) * 64], lhsT=ofmm[:, half * 8 + j, :, :].rearrange("p a b -> p (a b)"),
                         rhs=Fm[:, :], start=True, stop=True) for j in range(8)])
                OP("scalar", "copy", [("prR", half)], ["osb"], out=osb[0:NS, half * 512:(half + 1) * 512], in_=pr[half][0:NS, :])
            gn_post(NS, bon_s[:, :], "bon_s", NT)
            S.flush()
          S.barrier()


        if upto >= "C":
          with contextlib.ExitStack() as stC:
            sbC, psC = mk(stC)
            wC = sbC("wC", [128, 8, 3072], BF16)
            wda = sbC("wda", [128, 16, 1024], BF16)
            wdb = sbC("wdb", [128, 8, 1024], BF16)
            wout = sbC("wout", [128, 8, 1024], BF16)
            wpg = sbC("wpg", [128, 8, 1024], BF16)
            wple = sbC("wple", [128, 2, 1024], BF16)
            pgbc = sbC("pgbc", [128, 1024])
            fgbc = sbC("fgbc", [128, 1024])
            DMA(pgbc[:], ple_norm_g.partition_broadcast(128), [], ["pgbc"])
            DMA(fgbc[:], final_norm_g.partition_broadcast(128), [], ["fgbc"])
            with contextlib.ExitStack() as st:
                sb, ps = mk(st)
                stg = [sb("stgC%d" % i, [128, 2048]) for i in range(6)]
                load_w(wC, w_in[:, O5:NCOLS], D, 3072, stg, "wC")
                load_w(wda, w_down_a[:, :], 2048, 1024, stg, "wda")
                load_w(wdb, w_down_b[:, :], 1024, 1024, stg, "wdb")
                load_w(wout, w_out[:, :], 1024, 1024, stg, "wout")
                load_w(wpg, w_ple_gate[:, :], 1024, 1024, stg, "wpg")
                load_w(wple, w_ple[:, :], 256, 1024, stg, "wple")
            S.barrier()
            with contextlib.ExitStack() as st:
                sb, ps = mk(st)
                hTt = [sb("hTtC%d" % i, [128, 8, 128], BF16) for i in range(2)]
                yaT2 = [sb("yaTC%d" % i, [128, 16, 128], BF16) for i in range(2)]
                egs = [[sb("egC%d_%d" % (j, i), [128, 1024], BF16) for i in range(3)] for j in range(2)]
                junkC = sb("junkC", [128, 1024], BF16)
                ybt = sb("ybtC", [128, 1024])
                xt = [sb("xtC%d" % i, [128, 1024]) for i in range(2)]
                pt = [sb("ptC%d" % i, [128, 256]) for i in range(2)]
                f = {0: sb("fC0", [128, 1024]), 3: sb("fC3", [128, 1024])}
                ybg = sb("ybg", [128, 1024], BF16)
                ybT = sb("ybT", [128, 8, 128], BF16)
                mb = sb("mbC", [128, 1024], BF16)
                mT = sb("mTC", [128, 8, 128], BF16)
                hn = sb("hnC", [128, 1024], BF16)
                hnT = sb("hnTC", [128, 8, 128], BF16)
                pbf = sb("pbfC", [128, 256], BF16)
                ppT = sb("ppTC", [128, 2, 128], BF16)
                ssc = sb("sscC", [128, 2])
                pgC = [ps("pgC%d" % i, [128, 1024]) for i in range(3)]
                pTC = [ps("pTC%d" % i, [128, 512]) for i in range(2)]
                pTv = [p_[:, :].bitcast(BF16).rearrange("p (c f) -> p c f", c=8) for p_ in pTC]
                FK = [("fC", i) for i in range(4)]
                PG = [("pgC", i) for i in range(3)]
                tci = [0]

                def transp8(src, srck, rows, dst, dstk, n=8):
                    k = tci[0] % 2
                    tci[0] += 1
                    TR([srck, "identb"], [("pTC", k)],
                       [dict(out=pTv[k][:, c, :rows], in_=src[:rows, c * 128:(c + 1) * 128], identity=identb[:rows, :rows])
                        for c in range(n)])
                    OP("scalar", "copy", [("pTC", k)], [dstk], out=dst[:, 0:n, :rows], in_=pTv[k][:, 0:n, :rows])

                def loadC(t):
                    b = t % 2
                    if t < NT:
                        DMA(hTt[b][:], hT_all[:, :, t * 128:(t + 1) * 128], [], [("hTtC", b)])
                        DMA(xt[b][:], x[t * 128:(t + 1) * 128, :], [], [("xtC", b)])
                        DMA(pt[b][:], pp[t * 128:(t + 1) * 128, :], [], [("ptC", b)])
                    else:
                        DMA(hTt[b][:, :, 0:NS], hT_all[:, :, T:T + NS], [], [("hTtC", b)])
                        DMA(xt[b][0:NS, :], xs[:, :], [], [("xtC", b)])
                        DMA(pt[b][0:NS, :], psm[:, :], [], [("ptC", b)])

                def rms(src, srck, rows, col, gb, gbk, dst, dstk):
                    R = slice(0, rows)
                    OP("vector", "memset", [], [("sscC", col)], ap=ssc[R, col:col + 1], constant=0.0)
                    OP("scalar", "activation", [srck], ["junkC", ("sscC", col)], out=junkC[R, :], in_=src[R, :],
                       func=AF.Square, accum_out=ssc[R, col:col + 1])
                    rsqrt_chain(ssc[R, col:col + 1], ("sscC", col), rows, 1.0 / D, EPS)
                    OP("vector", "scalar_tensor_tensor", [srck, ("sscC", col), gbk], [dstk], out=dst[R, :], in0=src[R, :],
                       scalar=ssc[R, col:col + 1], in1=gb[R, :], op0=ALU.mult, op1=ALU.mult)

                def compC(t):
                    b = t % 2
                    rows = 128 if t < NT else NS
                    R = slice(0, rows)
                    hk = ("hTtC", b)
                    xk = ("xtC", b)
                    yaT = yaT2[b]
                    YAT = ("yaTC", b)
                    eg = egs[b]
                    EG = [("egC", b, i) for i in range(3)]
                    if t < NT:
                        DMA(yaT[:], yaT_all[t], [], [YAT])
                        DMA(ybt[:], yb_all[t], [], ["ybtC"])
                    else:
                        DMA(yaT[:, :, 0:NS], yaT_all[NT, :, :, 0:NS], [], [YAT])
                        DMA(ybt[0:NS, :], yb_all[NT, 0:NS, :], [], ["ybtC"])
                    for gi, (fn, fi) in enumerate(((AF.Silu, 0), (AF.Sigmoid, 1), (AF.Sigmoid, 2))):
                        for j in range(2):
                            col = gi * 1024 + j * 512
                            MM([hk, "wC"], [PG[gi]],
                               [dict(out=pgC[gi][R, j * 512:(j + 1) * 512], lhsT=hTt[b][:, kc, :rows],
                                     rhs=wC[:, kc, col:col + 512], start=(kc == 0), stop=(kc == 7)) for kc in range(8)])
                        OP("scalar", "activation", [PG[gi]], [EG[fi]], out=eg[fi][R, :], in_=pgC[gi][R, :], func=fn)
                    OP("vector", "tensor_tensor", ["ybtC", EG[0]], ["ybg"], out=ybg[R, :], in0=ybt[R, :], in1=eg[0][R, :], op=ALU.mult)
                    transp8(ybg, "ybg", rows, ybT, "ybT")
                    for j in range(2):
                        MM([YAT, "wda"], [PG[0]],
                           [dict(out=pgC[0][R, j * 512:(j + 1) * 512], lhsT=yaT[:, kc, :rows], rhs=wda[:, kc, j * 512:(j + 1) * 512],
                                 start=(kc == 0), stop=(kc == 15)) for kc in range(16)])
                    for j in range(2):
                        MM(["ybT", "wdb"], [PG[1]],
                           [dict(out=pgC[1][R, j * 512:(j + 1) * 512], lhsT=ybT[:, kc, :rows], rhs=wdb[:, kc, j * 512:(j + 1) * 512],
                                 start=(kc == 0), stop=(kc == 7)) for kc in range(8)])
                    OP("vector", "tensor_tensor", [PG[0], EG[1]], [FK[3]], out=f[3][R, :], in0=pgC[0][R, :], in1=eg[1][R, :], op=ALU.mult)
                    OP("vector", "tensor_tensor", [PG[1], EG[2]], [FK[0]], out=f[0][R, :], in0=pgC[1][R, :], in1=eg[2][R, :], op=ALU.mult)
                    OP("vector", "tensor_tensor", [FK[3], FK[0]], ["mbC"], out=mb[R, :], in0=f[3][R, :], in1=f[0][R, :], op=ALU.add)
                    transp8(mb, "mbC", rows, mT, "mTC")
                    for j in range(2):
                        MM(["mTC", "wout"], [PG[2]],
                           [dict(out=pgC[2][R, j * 512:(j + 1) * 512], lhsT=mT[:, kc, :rows], rhs=wout[:, kc, j * 512:(j + 1) * 512],
                                 start=(kc == 0), stop=(kc == 7)) for kc in range(8)])
                    OP("vector", "tensor_tensor", [PG[2], xk], [xk], out=xt[b][R, :], in0=xt[b][R, :], in1=pgC[2][R, :], op=ALU.add)
                    rms(xt[b], xk, rows, 0, pgbc, "pgbc", hn, "hnC")
                    transp8(hn, "hnC", rows, hnT, "hnTC")
                    for j in range(2):
                        MM(["hnTC", "wpg"], [PG[0]],
                           [dict(out=pgC[0][R, j * 512:(j + 1) * 512], lhsT=hnT[:, kc, :rows], rhs=wpg[:, kc, j * 512:(j + 1) * 512],
                                 start=(kc == 0), stop=(kc == 7)) for kc in range(8)])
                    OP("scalar", "activation", [PG[0]], [FK[0]], out=f[0][R, :], in_=pgC[0][R, :], func=AF.Sigmoid)
                    OP("gpsimd", "tensor_copy", [("ptC", b)], ["pbfC"], out=pbf[R, :], in_=pt[b][R, :])
                    transp8(pbf, "pbfC", rows, ppT, "ppTC", n=2)
                    for j in range(2):
                        MM(["ppTC", "wple"], [PG[1]],
                           [dict(out=pgC[1][R, j * 512:(j + 1) * 512], lhsT=ppT[:, kc, :rows], rhs=wple[:, kc, j * 512:(j + 1) * 512],
                                 start=(kc == 0), stop=(kc == 1)) for kc in range(2)])
                    OP("vector", "tensor_tensor", [PG[1], FK[0]], [FK[0]], out=f[0][R, :], in0=pgC[1][R, :], in1=f[0][R, :], op=ALU.mult)
                    OP("vector", "tensor_tensor", [xk, FK[0]], [xk], out=xt[b][R, :], in0=xt[b][R, :], in1=f[0][R, :], op=ALU.add)
                    rms(xt[b], xk, rows, 1, fgbc, "fgbc", xt[b], xk)
                    if t < NT:
                        DMAS(y_p[t * 128:(t + 1) * 128, :], xt[b][:, :], [xk], [("o_y", t)])
                    else:
                        DMA(y_s[:, :], xt[b][0:NS, :], [xk], [("o_y", t)])

                S.record()
                loadC(0)
                for t in range(NT + 1):
                    if t + 1 <= NT:
                        loadC(t + 1)
                    compC(t)
                S.flush()
          S.barrier()

        S.final_wait("sync")

        with nc.Block() as block:
            @block.sync
            def _(e):
                S.emit("sync", e)
                for (wk, val) in S.final[1]:
                    e.wait_ge(S.semof(wk), val)

            @block.scalar
            def _(e):
                S.emit("scalar", e)

            @block.vector
            def _(e):
                S.emit("vector", e)

            @block.gpsimd
            def _(e):
                S.emit("gpsimd", e)

            @block.tensor
            def _(e):
                S.emit("tensor", e)
    return nc


def make_consts():
    c = {}
    c["identb"] = np.eye(128, dtype=np.float32).astype(ml_dtypes.bfloat16)
    c["identf"] = np.eye(128, dtype=np.float32)
    half = 128
    inv = (np.float32(10000.0) ** (-np.arange(half, dtype=np.float32) / np.float32(half))).astype(np.float32)
    pos = np.concatenate([np.arange(T, dtype=np.float32), np.full(128, float(PAST), np.float32)])
    ang = (pos[:, None] * inv[None, :]).astype(np.float32).astype(np.float64)
    cs = np.stack([np.cos(ang), np.sin(ang)], axis=1).astype(np.float32)
    c["ropeq"] = np.ascontiguousarray(cs.reshape(NT + 1, 128, 2, 128))
    c["ropek"] = np.ascontiguousarray((cs.astype(np.float64) / 16.0).astype(np.float32).reshape(NT + 1, 128, 2, 128))
    i = np.arange(128, dtype=np.float64)
    g = np.array(G_RET, dtype=np.float64)
    dec = np.concatenate([g[None, :] ** (i[:, None] + 1.0), g[None, :] ** (-(i[:, None] + 1.0)),
                          g[None, :] ** (127.0 - i[:, None])], axis=1)
    c["dec"] = dec.astype(np.float32)
    p = np.arange(128)[:, None]
    f = np.arange(128)[None, :]
    c["masks"] = np.stack([(p <= f), (p < f), (p > f)]).astype(np.float32)
    sel = np.zeros((NS, NS, 128), np.float32)
    for b in range(NS):
        sel[b, b, :] = 1.0
    c["sel"] = sel
    c0 = -np.exp(-0.5)
    c["tri"] = (np.stack([(p <= f), (p < f)]).astype(np.float64) * c0).astype(np.float32)
    c["c0col"] = np.full((128, 1), c0, np.float32)
    c["mask3"] = np.ascontiguousarray(np.stack([(p <= f), (p < f), (p <= f)], axis=1).astype(np.float32))
    bI = np.arange(NS)[:, None, None]
    bbI = np.arange(8)[None, :, None]
    pI = np.arange(128)[None, None, :]
    c["sel2"] = (bI == (pI // 64) * 8 + bbI).astype(np.float32)
    c["hm"] = ((np.arange(NS)[:, None] // 8) == (np.arange(128)[None, :] // 64)).astype(np.float32)
    c["Em"] = ((np.arange(NS)[:, None] % 8) == np.arange(8)[None, :]).astype(np.float32)
    c["Fm"] = ((np.arange(128)[:, None] % 64) == np.arange(64)[None, :]).astype(np.float32)
    c["hmask"] = ((np.arange(128)[:, None] // 64) == np.arange(2)[None, :]).astype(np.float32)
    c["eye16"] = np.eye(16, dtype=np.float32).reshape(1, 256).astype(ml_dtypes.bfloat16)
    return c


def kernel(**inputs):
    inp = {k: np.asarray(v) for k, v in inputs.items()}
    nc = build_nc()
    consts = make_consts()
    in_maps = []
    for c in range(NCORES):
        m = dict(consts)
        sl = slice(c * NS, (c + 1) * NS)
        m["x"] = np.ascontiguousarray(inp["x_prompt"][c])
        m["xs"] = np.ascontiguousarray(inp["x_sample"][sl, 0, :])
        m["sshift"] = np.ascontiguousarray(inp["state_shift"][0, sl, :])
        m["norm_g"] = np.ascontiguousarray(inp["norm_g"])
        m["w_in"] = np.ascontiguousarray(inp["w_in"][0])
        m["sret"] = np.ascontiguousarray(inp["state_ret"][0, sl])
        m["srw"] = np.ascontiguousarray(inp["state_rwkv"][0, sl])
        for nm in ("rw_mu", "rw_w0", "rw_a0", "rw_k_k", "rw_k_a", "rw_ln_w", "rw_ln_b"):
            m[nm] = np.ascontiguousarray(inp[nm].reshape(1, -1))
        m["rw_r_k"] = np.ascontiguousarray(inp["rw_r_k"].reshape(1, 1024))
        for nm in ("w_down_a", "w_down_b", "w_out", "w_ple_gate", "w_ple"):
            m[nm] = np.ascontiguousarray(inp[nm][0])
        m["ple_norm_g"] = np.ascontiguousarray(inp["ple_norm_g"].reshape(1, -1))
        m["final_norm_g"] = np.ascontiguousarray(inp["final_norm_g"].reshape(1, -1))
        m["pp"] = np.ascontiguousarray(inp["p_prompt"][0, c])
        m["psm"] = np.ascontiguousarray(inp["p_sample"][0, sl, 0, :])
        m["rw_w2"] = np.ascontiguousarray(inp["rw_w2"][0])
        m["rw_a2"] = np.ascontiguousarray(inp["rw_a2"][0])
        in_maps.append(m)
    res = run_bass_kernel_spmd(nc, in_maps, core_ids=list(range(NCORES)))
    R = res.results
    shift_p = np.stack([R[c]["shift_p"][0] for c in range(NCORES)])[None]
    shift_s = np.concatenate([R[c]["shift_s"] for c in range(NCORES)], axis=0)[None]
    ret_p = np.stack([R[c]["ret_p"] for c in range(NCORES)])[None]
    ret_s = np.concatenate([R[c]["ret_s"] for c in range(NCORES)], axis=0)[None]
    if DEBUG:
        for nm in ("preB", "preF", "yb_all"):
            DBG[nm] = R[0][nm]
    rwkv_p = np.stack([R[c]["rwkv_p"] for c in range(NCORES)])[None]
    rwkv_s = np.concatenate([R[c]["rwkv_s"] for c in range(NCORES)], axis=0)[None]
    y_p = np.stack([R[c]["y_p"] for c in range(NCORES)])
    y_s = np.concatenate([R[c]["y_s"] for c in range(NCORES)], axis=0)[:, None, :]
    return y_p, y_s, ret_p, rwkv_p, shift_p, ret_s, rwkv_s, shift_s
```

```python
import contextlib
import numpy as np
import ml_dtypes
import concourse.bass as bass
import concourse.mybir as mybir
from concourse.bass_utils import run_bass_kernel_spmd

F32 = mybir.dt.float32
BF16 = mybir.dt.bfloat16
ALU = mybir.AluOpType
AF = mybir.ActivationFunctionType
AX = mybir.AxisListType

NCORES = 8
D = 1024
T = 2048
NT = 16
NS = 16
PAST = 16384
NCOLS = 12416
O1, O2, O3, O4 = 1024, 2048, 4096, 6144
O5 = O4 + 3200
O6 = O5 + 1024
O7 = O6 + 1024
EPS = 1e-6
GN_EPS = 1e-5 * 64
ENGS = ["sync", "scalar", "vector", "gpsimd", "tensor"]
NDS = 40


class Sched:
    def __init__(self, nc, stack):
        self.nc = nc
        self.ops = {e: [] for e in ENGS}
        self.cnt = {e: 0 for e in ENGS}
        self.sem = {e: stack.enter_context(nc.semaphore("se_" + e)) for e in ENGS}
        self.dsem = [stack.enter_context(nc.semaphore("sd%d" % i)) for i in range(NDS)]
        self.dcnt = [0] * NDS
        self.dnext = 0
        self.lastw = {}
        self.readers = {}
        self.seen = {e: {} for e in ENGS}
        self.rec = None

    def _deps(self, reads, writes):
        deps = []
        for k in reads:
            if k in self.lastw:
                deps.extend(self.lastw[k])
        for k in writes:
            if k in self.lastw:
                deps.extend(self.lastw[k])
            deps.extend(self.readers.get(k, []))
        return deps

    def _commit(self, tok, reads, writes):
        for k in reads:
            self.readers.setdefault(k, []).extend(tok if isinstance(tok, list) else [tok])
        toks = tok if isinstance(tok, list) else [tok]
        for k in writes:
            self.lastw[k] = list(toks)
            self.readers[k] = []

    def _waits(self, eng, deps, extra=()):
        need = {}
        for (sk, val) in list(deps) + list(extra):
            if val > need.get(sk, 0):
                need[sk] = val
        out = []
        for sk, val in need.items():
            if self.seen[eng].get(sk, 0) >= val:
                continue
            self.seen[eng][sk] = val
            out.append((sk, val))
        return out

    def op(self, eng, fn, reads=(), writes=(), cost=1.0):
        if self.rec is not None:
            self.rec.append(("op", eng, fn, list(reads), list(writes), cost))
            return None
        deps = self._deps(reads, writes)
        waits = self._waits(eng, deps)
        self.cnt[eng] += 1
        tok = (("e", eng), self.cnt[eng])
        self.ops[eng].append((waits, fn, ("e", eng)))
        self._commit(tok, reads, writes)
        return tok

    def dma(self, fn, reads=(), writes=(), eng="sync", cost=3.0):
        if self.rec is not None:
            self.rec.append(("dma", eng, fn, list(reads), list(writes), cost))
            return None
        i = self.dnext
        self.dnext = (self.dnext + 1) % NDS
        deps = self._deps(reads, writes)
        extra = [(("d", i), self.dcnt[i])] if self.dcnt[i] else []
        waits = self._waits(eng, deps, extra)
        self.dcnt[i] += 16
        tok = (("d", i), self.dcnt[i])
        self.ops[eng].append((waits, fn, ("d", i)))
        self._commit(tok, reads, writes)
        return tok

    def dma_multi(self, fns, reads=(), writes=(), eng="sync", cost=3.0):
        if self.rec is not None:
            self.rec.append(("dmam", eng, fns, list(reads), list(writes), cost))
            return None
        deps = self._deps(reads, writes)
        toks = []
        for fn in fns:
            i = self.dnext
            self.dnext = (self.dnext + 1) % NDS
            extra = [(("d", i), self.dcnt[i])] if self.dcnt[i] else []
            waits = self._waits(eng, deps, extra)
            self.dcnt[i] += 16
            toks.append((("d", i), self.dcnt[i]))
            self.ops[eng].append((waits, fn, ("d", i)))
        self._commit(toks, reads, writes)
        return toks

    def record(self):
        self.rec = []

    def flush(self):
        rec, self.rec = self.rec, None
        n = len(rec)
        lastw, readers = {}, {}
        succs = [[] for _ in range(n)]
        for i, r in enumerate(rec):
            reads, writes = r[3], r[4]
            ps = set()
            for k in reads:
                if k in lastw:
                    ps.add(lastw[k])
            for k in writes:
                if k in lastw:
                    ps.add(lastw[k])
                ps.update(readers.get(k, ()))
            ps.discard(i)
            for p in ps:
                succs[p].append(i)
            for k in reads:
                readers.setdefault(k, []).append(i)
            for k in writes:
                lastw[k] = i
                readers[k] = []
        prio = [0.0] * n
        for i in range(n - 1, -1, -1):
            m = 0.0
            for s_ in succs[i]:
                if prio[s_] > m:
                    m = prio[s_]
            prio[i] = m + rec[i][5]
        import heapq
        indeg = [0] * n
        for i in range(n):
            for s_ in succs[i]:
                indeg[s_] += 1
        ready = [0.0] * n
        heap = [(0.0, -prio[i], i) for i in range(n) if indeg[i] == 0]
        heapq.heapify(heap)
        eng_free = {}
        order = []
        while heap:
            r, _, i = heapq.heappop(heap)
            kind, eng, fn, reads, writes, cost = rec[i]
            start = max(r, eng_free.get(eng, 0.0))
            if kind == "op":
                fin = start + cost
                eng_free[eng] = fin
            else:
                fin = start + cost
                eng_free[eng] = start + 0.2
            order.append(i)
            for s_ in succs[i]:
                if fin > ready[s_]:
                    ready[s_] = fin
                indeg[s_] -= 1
                if indeg[s_] == 0:
                    heapq.heappush(heap, (ready[s_], -prio[s_], s_))
        assert len(order) == n
        for i in order:
            kind, eng, fn, reads, writes, cost = rec[i]
            if kind == "op":
                self.op(eng, fn, reads, writes)
            elif kind == "dma":
                self.dma(fn, reads, writes, eng=eng)
            else:
                self.dma_multi(fn, reads, writes, eng=eng)

    def barrier(self):
        for e in ENGS:
            waits = []
            for i in range(NDS):
                if self.dcnt[i]:
                    waits.append((("d", i), self.dcnt[i]))
            for e2 in ENGS:
                if self.cnt[e2]:
                    waits.append((("e", e2), self.cnt[e2]))
            waits = self._waits(e, waits)
            self.ops[e].append((waits, None, None))
        self.lastw = {}
        self.readers = {}

    def semof(self, sk):
        return self.sem[sk[1]] if sk[0] == "e" else self.dsem[sk[1]]

    def emit(self, eng_name, eng):
        for waits, fn, sk in self.ops[eng_name]:
            for (wk, val) in waits:
                eng.wait_ge(self.semof(wk), val)
            if fn is None:
                continue
            ins = fn(eng)
            ins.then_inc(self.semof(sk), 16 if sk[0] == "d" else 1)

    def final_wait(self, eng_name):
        waits = []
        for i in range(NDS):
            if self.dcnt[i]:
                waits.append((("d", i), self.dcnt[i]))
        for e in ENGS:
            if self.cnt[e] and e != eng_name:
                waits.append((("e", e), self.cnt[e]))
        self.final = (eng_name, waits)


DEBUG = False
DBG = {}
G_RET = [1.0 - 2.0 ** (-5.0 - h) for h in range(4)]


def build_nc(upto="C"):
    nc = bass.Bass("TRN2", target_bir_lowering=False)

    def din(name, shape, dt=F32):
        return nc.dram_tensor(name, list(shape), dt, kind="ExternalInput").ap()

    def dout(name, shape, dt=F32):
        return nc.dram_tensor(name, list(shape), dt, kind="ExternalOutput").ap()

    def dscr(name, shape, dt=F32):
        return nc.dram_tensor(name, list(shape), dt, kind="Internal").ap()

    x = din("x", [T, D])
    xs = din("xs", [NS, D])
    sshift = din("sshift", [NS, D])
    norm_g = din("norm_g", [1, D])
    w_in = din("w_in", [D, NCOLS])
    sret = din("sret", [NS, 4, 256, 512])
    identb_d = din("identb", [128, 128], BF16)
    identf_d = din("identf", [128, 128])
    ropeq_d = din("ropeq", [NT + 1, 128, 2, 128])
    ropek_d = din("ropek", [NT + 1, 128, 2, 128])
    dec_d = din("dec", [128, 12])
    masks_d = din("masks", [3, 128, 128])
    sel_d = din("sel", [NS, NS, 128])
    eye16_d = din("eye16", [1, 256], BF16)

    rw_mu = din("rw_mu", [1, 3200])
    rw_w0 = din("rw_w0", [1, 1024])
    rw_w2 = din("rw_w2", [64, 1024])
    rw_a0 = din("rw_a0", [1, 1024])
    rw_a2 = din("rw_a2", [64, 1024])
    rw_k_k = din("rw_k_k", [1, 1024])
    rw_k_a = din("rw_k_a", [1, 1024])
    rw_r_k = din("rw_r_k", [1, 1024])
    rw_ln_w = din("rw_ln_w", [1, 1024])
    rw_ln_b = din("rw_ln_b", [1, 1024])
    srw = din("srw", [NS, 16, 64, 64])
    tri_d = din("tri", [2, 128, 128])
    c0col_d = din("c0col", [128, 1])
    mask3_d = din("mask3", [128, 3, 128])
    sel2_d = din("sel2", [NS, 8, 128])
    hm_d = din("hm", [NS, 128])
    Em_d = din("Em", [NS, 8])
    Fm_d = din("Fm", [128, 64])
    hmask_d = din("hmask", [128, 2])
    rwkv_p = dout("rwkv_p", [16, 64, 64])
    rwkv_s = dout("rwkv_s", [NS, 16, 64, 64])
    preB = (dout if DEBUG else dscr)("preB", [NT, 128, 7168], BF16)
    preF = (dout if DEBUG else dscr)("preF", [NT + 1, 128, 1032])
    lastrow_d = dscr("lastrow_d", [1, 3200])
    yb_all = (dout if DEBUG else dscr)("yb_all", [NT + 1, 128, 1024])
    sampre = dscr("sampre", [NS, 7, 1024])

    w_down_a = din("w_down_a", [2048, 1024])
    w_down_b = din("w_down_b", [1024, 1024])
    w_out = din("w_out", [1024, 1024])
    w_ple_gate = din("w_ple_gate", [1024, 1024])
    w_ple = din("w_ple", [256, 1024])
    ple_norm_g = din("ple_norm_g", [1, 1024])
    final_norm_g = din("final_norm_g", [1, 1024])
    pp = din("pp", [T, 256])
    psm = din("psm", [NS, 256])
    y_p = dout("y_p", [T, D])
    y_s = dout("y_s", [NS, D])
    shift_p = dout("shift_p", [1, D])
    shift_s = dout("shift_s", [NS, D])
    ret_p = dout("ret_p", [4, 256, 512])
    ret_s = dout("ret_s", [NS, 4, 256, 512])

    hT_all = dscr("hT_all", [128, 8, T + NS], BF16)
    ssT_d = dscr("ssT_d", [128, 8, NS], BF16)
    yaT_all = dscr("yaT_all", [NT + 1, 128, 16, 128], BF16)

    with contextlib.ExitStack() as st0:
        S = Sched(nc, st0)

        HOP = 0.7
        ENGNS = {"vector": 1.0, "scalar": 1.1, "gpsimd": 2.6}

        def _free(ap):
            n = 1
            for d in ap.shape[1:]:
                n *= d
            return n

        def OP(eng, name, reads, writes, **kw):
            o = kw.get("out", kw.get("ap"))
            cost = HOP + ENGNS.get(eng, 1.0) * _free(o) / 1000.0
            return S.op(eng, lambda e, kw=kw, name=name: getattr(e, name)(**kw), reads, writes, cost=cost)

        def MM(reads, writes, mms):
            def fn(e, mms=mms):
                ins = None
                for m in mms:
                    ins = e.matmul(**m)
                return ins
            cost = HOP
            for m in mms:
                f = 4.0 if m["rhs"].dtype == F32 else 1.0
                cost += f * max(64, _free(m["rhs"])) * 0.00045
            return S.op("tensor", fn, reads, writes, cost=cost)

        def TR(reads, writes, trs):
            def fn(e, trs=trs):
                ins = None
                for m in trs:
                    ins = e.transpose(**m)
                return ins
            cost = HOP
            for m in trs:
                f = 4.0 if m["in_"].dtype == F32 else 1.0
                cost += f * max(64, m["in_"].shape[0]) * 0.00045
            return S.op("tensor", fn, reads, writes, cost=cost)

        def _dcost(out):
            try:
                n = out.shape[0] * _free(out) * (2 if out.dtype == BF16 else 4)
            except Exception:
                n = 1 << 18
            return 2.0 + n / 200e3

        def DMA(out, in_, reads, writes, eng="sync"):
            return S.dma(lambda e, out=out, in_=in_: e.dma_start(out=out, in_=in_), reads, writes, eng=eng, cost=_dcost(out))

        def DMAS(out, in_, reads, writes):
            return DMA(out, in_, reads, writes, eng="sync")

        def mk(st):
            def sb(name, shape, dt=F32):
                return st.enter_context(nc.sbuf_tensor(name, list(shape), dt))

            def ps(name, shape, dt=F32):
                return st.enter_context(nc.psum_tensor(name, list(shape), dt))
            return sb, ps

        sb0, ps0 = mk(st0)
        identb = sb0("identb_sb", [128, 128], BF16)
        identf = sb0("identf_sb", [128, 128])
        DMA(identb[:], identb_d[:, :], [], ["identb"])
        DMA(identf[:], identf_d[:, :], [], ["identf"])

        nhalf = sb0("nhalf", [128, 16])
        S.op("gpsimd", lambda e: e.memset(nhalf[:], -0.5), [], ["nhalf"])

        def rsqrt_chain(buf, key, rows, scale, bias):
            OP("vector", "tensor_scalar", [key], [key], out=buf, in0=buf, scalar1=scale, scalar2=bias,
               op0=ALU.mult, op1=ALU.add)
            OP("gpsimd", "tensor_tensor", [key, "nhalf"], [key], out=buf, in0=buf,
               in1=nhalf[:rows, 0:buf.shape[1]], op=ALU.pow)

        wkeys = {}

        def load_w(dst, src, K, N, stg, name, ci=[0]):
            keys = []
            CH = stg[0].shape[1]
            for kc in range(K // 128):
                for n0 in range(0, N, CH):
                    n1 = min(N, n0 + CH)
                    i = ci[0] % len(stg)
                    DMA(stg[i][:, :n1 - n0], src[kc * 128:(kc + 1) * 128, n0:n1], [], [("stg", i)])
                    eng = ["vector", "scalar", "vector", "scalar", "gpsimd"][ci[0] % 5]
                    k = (name, kc, n0)
                    if eng == "scalar":
                        OP(eng, "copy", [("stg", i)], [k], out=dst[:, kc, n0:n1], in_=stg[i][:, :n1 - n0])
                    else:
                        OP(eng, "tensor_copy", [("stg", i)], [k], out=dst[:, kc, n0:n1], in_=stg[i][:, :n1 - n0])
                    keys.append(k)
                    ci[0] += 1
            wkeys[name] = keys


        if upto >= "A":
          with contextlib.ExitStack() as stA:
            sbA, psA = mk(stA)
            wA = sbA("wA", [128, 8, 6144], BF16)
            dec = sbA("dec_sb", [128, 12])
            maskU = sbA("maskU_sb", [128, 128])
            DMA(dec[:], dec_d[:, :], [], ["dec"])
            DMA(maskU[:], masks_d[0], [], ["maskU"])
            with contextlib.ExitStack() as st:
                sb, ps = mk(st)
                S.record()
                stg = [sb("stgA%d" % i, [128, 2048]) for i in range(6)]
                load_w(wA, w_in[:, 0:6144], D, 6144, stg, "wA")
                gbc = sb("gbc", [128, D])
                DMA(gbc[:], norm_g.partition_broadcast(128), [], ["gbc"])
                xt = [sb("xt%d" % i, [128, D]) for i in range(2)]
                junk = sb("junk0", [128, D])
                ssq = sb("ssq", [128, 2])
                rstd = sb("rstd", [128, 2])
                hf = [sb("hf%d" % i, [128, D]) for i in range(2)]
                hb = [sb("hb%d" % i, [128, D], BF16) for i in range(2)]
                hTs = [sb("hTs%d" % i, [128, 8, 128], BF16) for i in range(2)]
                psT = [ps("psT%d" % i, [128, 8, 128], BF16) for i in range(2)]

                def tile_src(t):
                    if t < NT:
                        return x[t * 128:(t + 1) * 128, :], 128
                    if t == NT:
                        return xs[:, :], NS
                    return sshift[:, :], NS

                for t in range(NT + 2):
                    b = t % 2
                    src, rows = tile_src(t)
                    DMA(xt[b][:rows, :], src, [], [("xt", b)])
                    if t <= NT:
                        OP("vector", "memset", [], [("ssq", b)], ap=ssq[:rows, b:b + 1], constant=0.0)
                        OP("scalar", "activation", [("xt", b)], ["junk0", ("ssq", b)], out=junk[:rows, :],
                           in_=xt[b][:rows, :], func=AF.Square, accum_out=ssq[:rows, b:b + 1])
                        OP("vector", "tensor_copy", [("ssq", b)], [("rstd", b)], out=rstd[:rows, b:b + 1],
                           in_=ssq[:rows, b:b + 1])
                        rsqrt_chain(rstd[:rows, b:b + 1], ("rstd", b), rows, 1.0 / D, EPS)
                        OP("vector", "scalar_tensor_tensor", [("xt", b), ("rstd", b), "gbc"], [("hf", b)],
                           out=hf[b][:rows, :], in0=xt[b][:rows, :], scalar=rstd[:rows, b:b + 1],
                           in1=gbc[:rows, :], op0=ALU.mult, op1=ALU.mult)
                        if t == NT - 1:
                            DMA(shift_p[0:1, :], hf[b][127:128, :], [("hf", b)], ["o_shift_p"])
                        if t == NT:
                            DMA(shift_s[:, :], hf[b][:NS, :], [("hf", b)], ["o_shift_s"])
                        OP("vector", "tensor_copy", [("hf", b)], [("hb", b)], out=hb[b][:rows, :], in_=hf[b][:rows, :])
                    else:
                        OP("vector", "tensor_copy", [("xt", b)], [("hb", b)], out=hb[b][:rows, :], in_=xt[b][:rows, :])
                    TR([("hb", b), "identb"], [("psT", b)],
                       [dict(out=psT[b][:, kc, :rows], in_=hb[b][:rows, kc * 128:(kc + 1) * 128],
                             identity=identb[:rows, :rows]) for kc in range(8)])
                    OP("scalar", "copy", [("psT", b)], [("hTs", b)], out=hTs[b][:, :, :rows], in_=psT[b][:, :, :rows])
                    dst = hT_all[:, :, t * 128:t * 128 + rows] if t <= NT else ssT_d[:, :, :]
                    DMAS(dst, hTs[b][:, :, :rows], [("hTs", b)], [("hT_all", t)])
                S.flush()
            S.barrier()
            WA = ["wA"]

            def projA(hT_, rows, g, outp):
                MM([hT_[1]] + WA, [outp[1]],
                   [dict(out=outp[0], lhsT=hT_[0][:, kc, :rows], rhs=wA[:, kc, g * 512:(g + 1) * 512],
                         start=(kc == 0), stop=(kc == 7)) for kc in range(8)])

            def rope(src_ps, srck, rows, tab, tabk, rot, rotk, tmps):
                sv = src_ps.rearrange("p (h c f) -> p h c f", h=4, c=2)
                rv = rot.rearrange("p (h c f) -> p h c f", h=4, c=2)
                x1, x2 = sv[:, :, 0, :], sv[:, :, 1, :]
                cosb = tab[:, 0, :].unsqueeze(1).to_broadcast([rows, 4, 128])
                sinb = tab[:, 1, :].unsqueeze(1).to_broadcast([rows, 4, 128])
                tv = [(tt[0][:rows, :].rearrange("p (h f) -> p h f", h=4), tt[1]) for tt in tmps]
                OP("vector", "tensor_tensor", [srck, tabk], [tv[0][1]], out=tv[0][0], in0=x1, in1=cosb, op=ALU.mult)
                OP("vector", "tensor_tensor", [srck, tabk], [tv[1][1]], out=tv[1][0], in0=x2, in1=sinb, op=ALU.mult)
                OP("gpsimd", "tensor_tensor", [tv[0][1], tv[1][1]], [rotk], out=rv[:, :, 0, :], in0=tv[0][0],
                   in1=tv[1][0], op=ALU.subtract)
                OP("vector", "tensor_tensor", [srck, tabk], [tv[2][1]], out=tv[2][0], in0=x2, in1=cosb, op=ALU.mult)
                OP("vector", "tensor_tensor", [srck, tabk], [tv[3][1]], out=tv[3][0], in0=x1, in1=sinb, op=ALU.mult)
                OP("gpsimd", "tensor_tensor", [tv[2][1], tv[3][1]], [rotk], out=rv[:, :, 1, :], in0=tv[2][0],
                   in1=tv[3][0], op=ALU.add)

            def head_norm_gate(po_, pok, rows, sg_, sgk, ya_, yak, h, ssh, sshk):
                OP("vector", "memset", [], [sshk], ap=ssh, constant=0.0)
                OP("scalar", "activation", [pok], ["junkA", sshk], out=junkA[:rows, :], in_=po_,
                   func=AF.Square, accum_out=ssh)
                rsqrt_chain(ssh, sshk, rows, 1.0 / 512, EPS)
                OP("vector", "scalar_tensor_tensor", [pok, sshk, sgk], [yak],
                   out=ya_[:rows, h * 512:(h + 1) * 512], in0=po_, scalar=ssh,
                   in1=sg_[:rows, h * 512:(h + 1) * 512], op0=ALU.mult, op1=ALU.mult)

            junkA = sbA("junkA", [128, 512])

            with contextlib.ExitStack() as st:
                sb, ps = mk(st)
                hTt = [sb("hTt%d" % i, [128, 8, 128], BF16) for i in range(2)]
                rq = [sb("rq%d" % i, [128, 2, 128]) for i in range(2)]
                rk = [sb("rk%d" % i, [128, 2, 128]) for i in range(2)]
                rot = sb("rot", [128, 1024])
                tmps = [(sb("ropet%d" % i, [128, 512]), ("ropet", i)) for i in range(4)]
                qd = sb("qd", [128, 1024], BF16)
                kh = sb("kh", [128, 1024], BF16)
                kd2 = [sb("kd%d" % i, [128, 1024], BF16) for i in range(2)]
                qdT2 = [sb("qdT%d" % i, [128, 8, 128], BF16) for i in range(2)]
                khT2 = [sb("khT%d" % i, [128, 8, 128], BF16) for i in range(2)]
                vbf2 = [sb("vbf%d" % i, [128, 2048], BF16) for i in range(2)]
                sg2 = [sb("sg%d" % i, [128, 2048]) for i in range(2)]
                innerm = sb("innerm", [128, 4, 128], BF16)
                ya2 = [sb("ya%d" % i, [128, 2048], BF16) for i in range(2)]
                yT = sb("yT", [128, 16, 128], BF16)
                Sst = sb("Sst", [128, 8, 512])
                Sbf = sb("Sbf", [128, 8, 512], BF16)
                ssh = sb("ssh", [128, 4])
                pq = ps("pq", [128, 1024])
                pv = [ps("pv%d" % i, [128, 512]) for i in range(2)]
                pTf = ps("pTf", [128, 512])
                pT = pTf[:, :].bitcast(BF16).rearrange("p (c f) -> p c f", c=8)
                pin = ps("pin", [128, 512])
                po = [ps("po%d" % i, [128, 512]) for i in range(2)]

                OP("vector", "memset", [], [("Sst", c) for c in range(8)], ap=Sst[:], constant=0.0)
                OP("gpsimd", "memset", [], [("Sbf", c) for c in range(8)], ap=Sbf[:], constant=0.0)

                def loadA(t):
                    b = t % 2
                    DMA(hTt[b][:], hT_all[:, :, t * 128:(t + 1) * 128], [("hT_all", t)], [("hTt", b)])
                    DMA(rq[b][:], ropeq_d[t], [], [("rq", b)])
                    DMA(rk[b][:], ropek_d[t], [], [("rk", b)])

                pvi = [0]
                poi = [0]

                def computeA(t):
                    b = t % 2
                    kd, qdT, khT, vbf, sg, ya = kd2[b], qdT2[b], khT2[b], vbf2[b], sg2[b], ya2[b]
                    KD, QDT, KHT, VBF, SG, YA = ("kd", b), ("qdT", b), ("khT", b), ("vbf", b), ("sg", b), ("ya", b)
                    hT_ = (hTt[b], ("hTt", b))
                    for g in range(2):
                        projA(hT_, 128, g, (pq[:, g * 512:(g + 1) * 512], "pq"))
                    rope(pq[:, :], "pq", 128, rq[b], ("rq", b), rot[:, :], "rot", tmps)
                    OP("gpsimd", "tensor_tensor", ["rot", "dec"], ["qd"],
                       out=qd[:, :].rearrange("p (h f) -> p h f", h=4),
                       in0=rot[:, :].rearrange("p (h f) -> p h f", h=4),
                       in1=dec[:, 0:4].unsqueeze(2).to_broadcast([128, 4, 256]), op=ALU.mult)
                    for g in range(2):
                        projA(hT_, 128, 2 + g, (pq[:, g * 512:(g + 1) * 512], "pq"))
                    rope(pq[:, :], "pq", 128, rk[b], ("rk", b), rot[:, :], "rot", tmps)
                    OP("gpsimd", "tensor_tensor", ["rot", "dec"], ["kh"],
                       out=kh[:, :].rearrange("p (h f) -> p h f", h=4),
                       in0=rot[:, :].rearrange("p (h f) -> p h f", h=4),
                       in1=dec[:, 4:8].unsqueeze(2).to_broadcast([128, 4, 256]), op=ALU.mult)
                    OP("gpsimd", "tensor_tensor", ["rot", "dec"], [KD],
                       out=kd[:, :].rearrange("p (h f) -> p h f", h=4),
                       in0=rot[:, :].rearrange("p (h f) -> p h f", h=4),
                       in1=dec[:, 8:12].unsqueeze(2).to_broadcast([128, 4, 256]), op=ALU.mult)
                    TR(["qd", "identb"], ["pT"],
                       [dict(out=pT[:, c, :], in_=qd[:, c * 128:(c + 1) * 128], identity=identb[:, :]) for c in range(8)])
                    OP("scalar", "copy", ["pT"], [QDT], out=qdT[:], in_=pT)
                    TR(["kh", "identb"], ["pT"],
                       [dict(out=pT[:, c, :], in_=kh[:, c * 128:(c + 1) * 128], identity=identb[:, :]) for c in range(8)])
                    OP("scalar", "copy", ["pT"], [KHT], out=khT[:], in_=pT)
                    for g in range(4):
                        k = pvi[0] % 2
                        pvi[0] += 1
                        projA(hT_, 128, 4 + g, (pv[k][:, :], ("pv", k)))
                        OP("scalar", "copy", [("pv", k)], [VBF], out=vbf[:, g * 512:(g + 1) * 512], in_=pv[k][:, :])
                    for g in range(4):
                        k = pvi[0] % 2
                        pvi[0] += 1
                        projA(hT_, 128, 8 + g, (pv[k][:, :], ("pv", k)))
                        OP("scalar", "activation", [("pv", k)], [SG], out=sg[:, g * 512:(g + 1) * 512],
                           in_=pv[k][:, :], func=AF.Silu)
                    MM([KHT, QDT], ["pin"],
                       [dict(out=pin[:, h * 128:(h + 1) * 128], lhsT=khT[:, 2 * h + c, :], rhs=qdT[:, 2 * h + c, :],
                             start=(c == 0), stop=(c == 1)) for h in range(4) for c in range(2)])
                    OP("vector", "tensor_tensor", ["pin", "maskU"], ["innerm"], out=innerm[:],
                       in0=pin[:, :].rearrange("p (h f) -> p h f", h=4),
                       in1=maskU[:, :].unsqueeze(1).to_broadcast([128, 4, 128]), op=ALU.mult)
                    for h in range(4):
                        k = poi[0] % 2
                        poi[0] += 1
                        MM(["innerm", VBF, QDT, ("Sbf", 2 * h), ("Sbf", 2 * h + 1)], [("po", k)],
                           [dict(out=po[k][:, :], lhsT=innerm[:, h, :], rhs=vbf[:, h * 512:(h + 1) * 512],
                                 start=True, stop=False)] +
                           [dict(out=po[k][:, :], lhsT=qdT[:, 2 * h + c, :], rhs=Sbf[:, 2 * h + c, :],
                                 start=False, stop=(c == 1)) for c in range(2)])
                        head_norm_gate(po[k][:, :], ("po", k), 128, sg, SG, ya, YA, h, ssh[:, h:h + 1], ("ssh", h))
                    for h in range(4):
                        for c in range(2):
                            k = pvi[0] % 2
                            pvi[0] += 1
                            ch = 2 * h + c
                            MM([KD, VBF], [("pv", k)],
                               [dict(out=pv[k][:, :], lhsT=kd[:, ch * 128:(ch + 1) * 128],
                                     rhs=vbf[:, h * 512:(h + 1) * 512], start=True, stop=True)])
                            OP("vector", "scalar_tensor_tensor", [("pv", k), ("Sst", ch)], [("Sst", ch)],
                               out=Sst[:, ch, :], in0=Sst[:, ch, :], scalar=float(G_RET[h] ** 128),
                               in1=pv[k][:, :], op0=ALU.mult, op1=ALU.add)
                            OP("scalar", "copy", [("Sst", ch)], [("Sbf", ch)], out=Sbf[:, ch, :], in_=Sst[:, ch, :])
                    for half in range(2):
                        TR([YA, "identb"], ["pT"],
                           [dict(out=pT[:, c, :], in_=ya[:, (half * 8 + c) * 128:(half * 8 + c + 1) * 128],
                                 identity=identb[:, :]) for c in range(8)])
                        OP("scalar", "copy", ["pT"], ["yT"], out=yT[:, half * 8:(half + 1) * 8, :], in_=pT)
                    DMAS(yaT_all[t], yT[:], ["yT"], [("yaT_all", t)])

                S.record()
                loadA(0)
                for t in range(NT):
                    if t + 1 < NT:
                        loadA(t + 1)
                    computeA(t)
                DMA(ret_p.rearrange("h (c p) v -> p (h c) v", c=2), Sst[:], [("Sst", c) for c in range(8)], ["o_ret_p"])
                S.flush()
            S.barrier()

            with contextlib.ExitStack() as st:
                sb, ps = mk(st)
                hTs_ = sb("hTsA", [128, 8, NS], BF16)
                rq = sb("rqs", [NS, 2, 128])
                rk = sb("rks", [NS, 2, 128])
                rotq = sb("rotq", [NS, 1024])
                rotk = sb("rotk", [NS, 1024])
                tmps = [(sb("ropets%d" % i, [NS, 512]), ("ropet", i)) for i in range(4)]
                qb = sb("qbs", [NS, 1024], BF16)
                qT = sb("qTs", [128, 8, NS], BF16)
                kTf = sb("kTfs", [128, 8, NS])
                vs = sb("vss", [NS, 2048])
                sg = sb("sgs", [NS, 2048])
                ya = sb("yas", [NS, 2048], BF16)
                yT = sb("yTs", [128, 16, NS], BF16)
                ssh = sb("sshs", [NS, 4])
                sel = sb("sel_sb", [NS, NS, 128])
                eye16 = sb("eye16_sb", [128, NS, NS], BF16)
                qmask = sb("qmask", [128, 8, NS, NS], BF16)
                sin_ = [sb("sin%d" % i, [128, 2, 512]) for i in range(4)]
                snew = [sb("snew%d" % i, [128, 2, 512]) for i in range(2)]
                snb = [sb("snb%d" % i, [128, 2, 512], BF16) for i in range(2)]
                pq = ps("pqs", [128, 1024])
                pvb = [ps("pvb%d" % i, [128, 512]) for i in range(2)]
                pos = [ps("pos%d" % i, [128, 512]) for i in range(4)]
                pT = pq[:, 0:512].bitcast(BF16).rearrange("p (c f) -> p c f", c=8)
                pTf = pq[:, :].rearrange("p (c f) -> p c f", c=8)

                S.record()
                DMA(hTs_[:], hT_all[:, :, T:T + NS], [("hT_all", NT)], ["hTsA"])
                DMA(rq[:], ropeq_d[NT, 0:NS], [], ["rqs"])
                DMA(rk[:], ropek_d[NT, 0:NS], [], ["rks"])
                DMA(sel[:], sel_d[:, :, :], [], ["sel"])
                DMA(eye16[:].rearrange("p a b -> p (a b)"), eye16_d.partition_broadcast(128), [], ["eye16"])
                hT_ = (hTs_, "hTsA")
                for g in range(2):
                    projA(hT_, NS, g, (pq[:NS, g * 512:(g + 1) * 512], "pq"))
                rope(pq[:NS, :], "pq", NS, rq, "rqs", rotq[:, :], "rotq", tmps)
                for g in range(2):
                    projA(hT_, NS, 2 + g, (pq[:NS, g * 512:(g + 1) * 512], "pq"))
                rope(pq[:NS, :], "pq", NS, rk, "rks", rotk[:, :], "rotk", tmps)
                OP("gpsimd", "tensor_copy", ["rotq"], ["qbs"], out=qb[:], in_=rotq[:])
                TR(["qbs", "identb"], ["pq"],
                   [dict(out=pT[:, c, :NS], in_=qb[:, c * 128:(c + 1) * 128], identity=identb[:NS, :NS]) for c in range(8)])
                OP("scalar", "copy", ["pq"], ["qTs"], out=qT[:], in_=pT[:, :, :NS])
                TR(["rotk", "identf"], ["pq"],
                   [dict(out=pTf[:, c, :NS], in_=rotk[:, c * 128:(c + 1) * 128], identity=identf[:NS, :NS]) for c in range(8)])
                OP("scalar", "copy", ["pq"], ["kTfs"], out=kTf[:], in_=pTf[:, :, :NS])
                for g in range(4):
                    k = g % 2
                    projA(hT_, NS, 4 + g, (pvb[k][:NS, :], ("pvb", k)))
                    OP("scalar", "copy", [("pvb", k)], ["vss"], out=vs[:, g * 512:(g + 1) * 512], in_=pvb[k][:NS, :])
                for g in range(4):
                    k = g % 2
                    projA(hT_, NS, 8 + g, (pvb[k][:NS, :], ("pvb", k)))
                    OP("scalar", "activation", [("pvb", k)], ["sgs"], out=sg[:, g * 512:(g + 1) * 512],
                       in_=pvb[k][:NS, :], func=AF.Silu)
                OP("vector", "tensor_tensor", ["qTs", "eye16"], ["qmask"], out=qmask[:],
                   in0=qT[:].unsqueeze(2).to_broadcast([128, 8, NS, NS]),
                   in1=eye16[:].unsqueeze(1).to_broadcast([128, 8, NS, NS]), op=ALU.mult)

                def load_s(i):
                    b, h = divmod(i, 4)
                    DMA(sin_[i % 4][:], sret[b, h].rearrange("(c p) v -> p c v", c=2), [], [("sin", i % 4)])

                load_s(0)
                load_s(1)
                load_s(2)

                def bc_v(i):
                    b, h = divmod(i, 4)
                    k = i % 2
                    MM(["sel", "vss"], [("pvb", k)],
                       [dict(out=pvb[k][:, :], lhsT=sel[:, b, :], rhs=vs[:, h * 512:(h + 1) * 512], start=True, stop=True)])

                bc_v(0)
                for i in range(NS * 4):
                    b, h = divmod(i, 4)
                    if i + 3 < NS * 4:
                        load_s(i + 3)
                    if i + 1 < NS * 4:
                        bc_v(i + 1)
                    si = sin_[i % 4]
                    k = i % 2
                    OP("scalar", "mul", [("sin", i % 4)], [("sin", i % 4)], out=si[:], in_=si[:], mul=float(G_RET[h]))
                    for c in range(2):
                        OP("vector", "scalar_tensor_tensor", [("pvb", k), "kTfs", ("sin", i % 4)], [("snew", k)],
                           out=snew[k][:, c, :], in0=pvb[k][:, :], scalar=kTf[:, 2 * h + c, b:b + 1],
                           in1=si[:, c, :], op0=ALU.mult, op1=ALU.add)
                    DMA(ret_s[b, h].rearrange("(c p) v -> p c v", c=2), snew[k][:], [("snew", k)], [("o_ret_s", i)])
                    OP("scalar", "copy", [("snew", k)], [("snb", k)], out=snb[k][:], in_=snew[k][:])
                    MM([("snb", k), "qmask"], [("pos", h)],
                       [dict(out=pos[h][:NS, :], lhsT=qmask[:, 2 * h + c, b, :], rhs=snb[k][:, c, :],
                             start=(b == 0 and c == 0), stop=(b == NS - 1 and c == 1)) for c in range(2)])
                for h in range(4):
                    head_norm_gate(pos[h][:NS, :], ("pos", h), NS, sg, "sgs", ya, "yas", h, ssh[:, h:h + 1], ("sshs", h))
                for half in range(2):
                    TR(["yas", "identb"], ["pq"],
                       [dict(out=pT[:, c, :NS], in_=ya[:, (half * 8 + c) * 128:(half * 8 + c + 1) * 128],
                             identity=identb[:NS, :NS]) for c in range(8)])
                    OP("scalar", "copy", ["pq"], ["yTs"], out=yT[:, half * 8:(half + 1) * 8, :], in_=pT[:, :, :NS])
                DMAS(yaT_all[NT, :, :, 0:NS], yT[:], ["yTs"], [("yaT_all", NT)])
                S.flush()
            S.barrier()


        C0 = -float(np.exp(-0.5))
        if upto >= "B":
          with contextlib.ExitStack() as stB:
            sbB, psB = mk(stB)
            wB = sbB("wB", [128, 8, 3200], BF16)
            mubc = sbB("mubc", [128, 3200])
            kkbc = sbB("kkbc", [128, 1024])
            kabc = sbB("kabc", [128, 1024])
            rkbc = sbB("rkbc", [128, 1024])
            w2x = sbB("w2x", [65, 1024], BF16)
            a2x = sbB("a2x", [65, 1024], BF16)
            tri = sbB("tri_sb", [128, 2, 128])
            c0col = sbB("c0col_sb", [128, 1])
            DMA(mubc[:], rw_mu.partition_broadcast(128), [], ["mubc"])
            DMA(kkbc[:], rw_k_k.partition_broadcast(128), [], ["kkbc"])
            DMA(kabc[:], rw_k_a.partition_broadcast(128), [], ["kabc"])
            DMA(rkbc[:], rw_r_k.partition_broadcast(128), [], ["rkbc"])
            DMA(tri[:, 0, :], tri_d[0], [], ["tri"])
            DMA(tri[:, 1, :], tri_d[1], [], ["tri"])
            DMA(c0col[:], c0col_d[:, :], [], ["c0col"])
            with contextlib.ExitStack() as st:
                sb, ps = mk(st)
                stg = [sb("stgB%d" % i, [128, 2048]) for i in range(6)]
                load_w(wB, w_in[:, O4:O5], D, 3200, stg, "wB")
                DMA(stg[0][0:64, 0:1024], rw_w2[:, :], [("stg", 0)], [("stg", 0)])
                DMA(stg[0][64:65, 0:1024], rw_w0[:, :], [("stg", 0)], [("stg", 0)])
                OP("vector", "tensor_copy", [("stg", 0)], ["w2x"], out=w2x[:], in_=stg[0][0:65, 0:1024])
                DMA(stg[1][0:64, 0:1024], rw_a2[:, :], [("stg", 1)], [("stg", 1)])
                DMA(stg[1][64:65, 0:1024], rw_a0[:, :], [("stg", 1)], [("stg", 1)])
                OP("vector", "tensor_copy", [("stg", 1)], ["a2x"], out=a2x[:], in_=stg[1][0:65, 0:1024])
            S.barrier()

            with contextlib.ExitStack() as st:
                sb, ps = mk(st)
                hTt = [sb("hTtB%d" % i, [128, 8, 128], BF16) for i in range(3)]
                ssTt = sb("ssTt", [128, 8, NS], BF16)
                sh = sb("sh", [128, 3200])
                zs = [sb("z%d" % i, [128, 3200]) for i in range(2)]
                lo = sb("lo", [128, 128], BF16)
                loT = sb("loT", [65, 2, 128], BF16)
                sgw = sb("sgw", [128, 1024])
                av = sb("av", [128, 1024])
                kk = sb("kk", [128, 1024])
                k2 = sb("k2", [128, 1024])
                bb = sb("bb", [128, 1024])
                tmpf = sb("tmpf", [128, 1024])
                e1 = sb("e1", [128, 1024])
                e2 = sb("e2", [128, 1024])
                e3 = sb("e3", [128, 1024])
                rt = sb("rt", [128, 1024], BF16)
                kt = sb("kt", [128, 1024], BF16)
                small = sb("smallB", [128, 64])
                packB = [sb("packB%d" % i, [128, 7168], BF16) for i in range(2)]
                packF = [sb("packF%d" % i, [128, 1032]) for i in range(2)]
                pg = [ps("pgB%d" % i, [128, 1024]) for i in range(3)]
                pTf = ps("pTB", [128, 512])
                pT = pTf[:, :].bitcast(BF16).rearrange("p (c f) -> p c f", c=8)
                pgc = ps("pgc", [128, 512])
                OP("vector", "memset", [], ["loT"], ap=loT[64:65, :, :], constant=1.0)

                pgi = [0]

                def nextpg():
                    k = pgi[0] % 3
                    pgi[0] += 1
                    return pg[k], ("pgB", k)

                def projB(hT_, hk, rows, dst, dstk):
                    for g in range(4):
                        p_, pk_ = nextpg()
                        for j in range(2):
                            c0_ = g * 1024 + j * 512
                            if c0_ >= 3200:
                                continue
                            c1_ = min(3200, c0_ + 512)
                            MM([hk, "wB"], [pk_],
                               [dict(out=p_[:rows, j * 512:j * 512 + (c1_ - c0_)], lhsT=hT_[:, kc, :rows],
                                     rhs=wB[:, kc, c0_:c1_], start=(kc == 0), stop=(kc == 7)) for kc in range(8)])
                        w_ = min(3200, (g + 1) * 1024) - g * 1024
                        OP("scalar", "copy", [pk_], [dstk], out=dst[:rows, g * 1024:g * 1024 + w_], in_=p_[:rows, :w_])

                def preB_s1(t):
                    b = t % 2
                    rows = 128 if t < NT else NS
                    R = slice(0, rows)
                    z = zs[b]
                    zk_ = ("z", b)
                    projB(hTt[t % 3], ("hTtB", t % 3), rows, sh, "sh")
                    if t < NT:
                        fns = [lambda e, z=z: e.dma_start(out=z[1:113, :], in_=sh[0:112, :]),
                               lambda e, z=z: e.dma_start(out=z[113:128, :], in_=sh[112:127, :])]
                        if t > 0:
                            fns.append(lambda e, z=z: e.dma_start(out=z[0:1, :], in_=lastrow_d[:, :]))
                        S.dma_multi(fns, ["sh", "lastrow"], [zk_])
                        if t == 0:
                            OP("vector", "memset", [], [zk_], ap=z[0:1, :], constant=0.0)
                        DMA(lastrow_d[:, :], sh[127:128, :], ["sh"], ["lastrow"])
                    else:
                        projB(ssTt, "ssTt", rows, z, zk_)
                    OP("vector", "tensor_tensor", [zk_, "sh"], [zk_], out=z[R, :], in0=z[R, :], in1=sh[R, :], op=ALU.subtract)
                    OP("vector", "tensor_tensor", [zk_, "mubc"], [zk_], out=z[R, :], in0=z[R, :], in1=mubc[R, :], op=ALU.mult)
                    OP("vector", "tensor_tensor", [zk_, "sh"], [zk_], out=z[R, :], in0=z[R, :], in1=sh[R, :], op=ALU.add)

                def preB_s2(t):
                    b = t % 2
                    rows = 128 if t < NT else NS
                    R = slice(0, rows)
                    pb_, pf_ = packB[b], packF[b]
                    z = zs[b]
                    zk_ = ("z", b)
                    zr, zk, zv = z[R, 0:1024], z[R, 1024:2048], z[R, 2048:3072]
                    OP("scalar", "activation", [zk_], ["lo"], out=lo[R, 0:64], in_=z[R, 3072:3136], func=AF.Tanh)
                    OP("vector", "tensor_copy", [zk_], ["lo"], out=lo[R, 64:128], in_=z[R, 3136:3200])
                    TR(["lo", "identb"], ["pTB"],
                       [dict(out=pT[0:64, j, :rows], in_=lo[R, j * 64:(j + 1) * 64], identity=identb[R, R]) for j in range(2)])
                    OP("scalar", "copy", ["pTB"], ["loT"], out=loT[0:64, :, :rows], in_=pT[0:64, 0:2, :rows])
                    pw_, pwk = nextpg()
                    MM(["loT", "w2x"], [pwk],
                       [dict(out=pw_[R, j * 512:(j + 1) * 512], lhsT=loT[0:65, 0, :rows], rhs=w2x[0:65, j * 512:(j + 1) * 512],
                             start=True, stop=True) for j in range(2)])
                    OP("scalar", "activation", [pwk], ["sgw"], out=sgw[R, :], in_=pw_[R, :], func=AF.Sigmoid)
                    pa_, pak = nextpg()
                    MM(["loT", "a2x"], [pak],
                       [dict(out=pa_[R, j * 512:(j + 1) * 512], lhsT=loT[0:65, 1, :rows], rhs=a2x[0:65, j * 512:(j + 1) * 512],
                             start=True, stop=True) for j in range(2)])
                    OP("scalar", "activation", [pak], ["av"], out=av[R, :], in_=pa_[R, :], func=AF.Sigmoid)
                    OP("gpsimd", "tensor_tensor", [zk_, "kkbc"], ["kk"], out=kk[R, :], in0=zk, in1=kkbc[R, :], op=ALU.mult)
                    OP("scalar", "activation", ["kk"], ["tmpf"], out=tmpf[R, :], in_=kk[R, :], func=AF.Square)
                    OP("vector", "tensor_reduce", ["tmpf"], ["smallB"], out=small[R, 0:16],
                       in_=tmpf[R, :].rearrange("p (h f) -> p h f", h=16), axis=AX.X, op=ALU.add)
                    OP("vector", "tensor_scalar_max", ["smallB"], ["smallB"], out=small[R, 0:16], in0=small[R, 0:16], scalar1=1e-24)
                    OP("gpsimd", "tensor_tensor", ["smallB", "nhalf"], ["smallB"], out=small[R, 0:16], in0=small[R, 0:16],
                       in1=nhalf[R, 0:16], op=ALU.pow)
                    OP("vector", "tensor_tensor", ["kk", "smallB"], ["kk"], out=kk[R, :].rearrange("p (h f) -> p h f", h=16),
                       in0=kk[R, :].rearrange("p (h f) -> p h f", h=16),
                       in1=small[R, 0:16].unsqueeze(2).to_broadcast([rows, 16, 64]), op=ALU.mult)
                    OP("vector", "scalar_tensor_tensor", ["av", "kabc"], ["tmpf"], out=tmpf[R, :], in0=av[R, :], scalar=-1.0,
                       in1=kabc[R, :], op0=ALU.add, op1=ALU.mult)
                    OP("vector", "scalar_tensor_tensor", ["tmpf", zk_], ["k2"], out=k2[R, :], in0=tmpf[R, :], scalar=1.0,
                       in1=zk, op0=ALU.add, op1=ALU.mult)
                    OP("gpsimd", "tensor_tensor", ["av", "kk"], ["bb"], out=bb[R, :], in0=av[R, :], in1=kk[R, :], op=ALU.mult)
                    OP("gpsimd", "tensor_tensor", [zk_, "k2"], ["tmpf"], out=tmpf[R, :], in0=zr, in1=k2[R, :], op=ALU.mult)
                    OP("gpsimd", "tensor_tensor", ["tmpf", "rkbc"], ["tmpf"], out=tmpf[R, :], in0=tmpf[R, :], in1=rkbc[R, :], op=ALU.mult)
                    OP("vector", "tensor_reduce", ["tmpf"], ["smallB2"], out=small[R, 16:32],
                       in_=tmpf[R, :].rearrange("p (h f) -> p h f", h=16), axis=AX.X, op=ALU.add)
                    if t < NT:
                        OP("gpsimd", "tensor_tensor", [zk_, "smallB2"], [("packF", b)],
                           out=pf_[R, 0:1024].rearrange("p (h f) -> p h f", h=16),
                           in0=zv.rearrange("p (h f) -> p h f", h=16),
                           in1=small[R, 16:32].unsqueeze(2).to_broadcast([rows, 16, 64]), op=ALU.mult)
                        OP("scalar", "copy", [zk_], [("packB", b)], out=pb_[R, 6144:7168], in_=zv)
                        pc_, pck = nextpg()
                        MM(["tri", "sgw"], [pck],
                           [dict(out=pc_[:, j * 512:(j + 1) * 512], lhsT=tri[:, 0, :], rhs=sgw[:, j * 512:(j + 1) * 512],
                                 start=True, stop=True) for j in range(2)])
                        OP("scalar", "activation", [pck], ["e1"], out=e1[:], in_=pc_[:, :], func=AF.Exp)
                        OP("scalar", "activation", [pck], ["e2"], out=e2[:], in_=pc_[:, :], func=AF.Exp, scale=-1.0)
                        ps_, psk = nextpg()
                        MM(["tri", "sgw"], [psk],
                           [dict(out=ps_[:, j * 512:(j + 1) * 512], lhsT=tri[:, 1, :], rhs=sgw[:, j * 512:(j + 1) * 512],
                                 start=True, stop=True) for j in range(2)])
                        OP("scalar", "activation", [psk], ["e3"], out=e3[:], in_=ps_[:, :], func=AF.Exp)
                        MM(["sgw", "c0col"], ["pgc"],
                           [dict(out=pgc[:, hh:hh + 1], lhsT=sgw[:, hh * 128:(hh + 1) * 128], rhs=c0col[:, 0:1],
                                 start=True, stop=True) for hh in range(8)])
                        OP("scalar", "activation", ["pgc"], [("packF", b)], out=pf_[:, 1024:1032], in_=pgc[:, 0:8], func=AF.Exp)
                        OP("vector", "tensor_tensor", [zk_, "e1"], ["rt"], out=rt[:], in0=zr, in1=e1[:], op=ALU.mult)
                        OP("gpsimd", "tensor_tensor", ["k2", "e2"], [("packB", b)], out=pb_[:, 5120:6144], in0=k2[:], in1=e2[:], op=ALU.mult)
                        OP("vector", "tensor_tensor", ["bb", "e2"], [("packB", b)], out=pb_[:, 4096:5120], in0=bb[:], in1=e2[:], op=ALU.mult)
                        OP("gpsimd", "tensor_tensor", ["kk", "e3"], ["kt"], out=kt[:], in0=kk[:], in1=e3[:], op=ALU.mult)
                        RKv = pb_[:, 0:2048].rearrange("p (c j f) -> p c j f", c=8, j=2)
                        for (src, srck, dst) in ((kt, "kt", RKv[:, :, 0, :]), (rt, "rt", RKv[:, :, 1, :]),
                                                 (pb_[:, 4096:5120], ("packB", b), pb_[:, 2048:3072].rearrange("p (c f) -> p c f", c=8)),
                                                 (pb_[:, 5120:6144], ("packB", b), pb_[:, 3072:4096].rearrange("p (c f) -> p c f", c=8))):
                            TR([srck, "identb"], ["pTB"],
                               [dict(out=pT[:, c, :], in_=src[:, c * 128:(c + 1) * 128], identity=identb[:, :]) for c in range(8)])
                            OP("scalar", "copy", ["pTB"], [("packB", b)], out=dst, in_=pT)
                        DMA(preB[t], pb_[:], [("packB", b)], [("preB", t)])
                        DMA(preF[t], pf_[:], [("packF", b)], [("preF", t)])
                    else:
                        OP("gpsimd", "tensor_tensor", [zk_, "smallB2"], [("packF", b)],
                           out=pf_[R, 0:1024].rearrange("p (h f) -> p h f", h=16),
                           in0=zv.rearrange("p (h f) -> p h f", h=16),
                           in1=small[R, 16:32].unsqueeze(2).to_broadcast([rows, 16, 64]), op=ALU.mult)
                        DMA(preF[t, 0:NS, 0:1024], pf_[R, 0:1024], [("packF", b)], [("preF", t)])
                        OP("scalar", "activation", ["sgw"], ["e1"], out=e1[R, :], in_=sgw[R, :], func=AF.Exp, scale=C0)
                        DMA(sampre[:, 0, :], kk[R, :], ["kk"], [("sampre", 0)])
                        DMA(sampre[:, 1, :], e1[R, :], ["e1"], [("sampre", 1)])
                        DMA(sampre[:, 2, :], bb[R, :], ["bb"], [("sampre", 2)])
                        DMA(sampre[:, 3, :], k2[R, :], ["k2"], [("sampre", 3)])
                        DMA(sampre[:, 4, :], zr, [zk_], [("sampre", 4)])
                        DMA(sampre[:, 5, :], zv, [zk_], [("sampre", 5)])

                def loadB(t):
                    b = t % 3
                    if t < NT:
                        DMA(hTt[b][:], hT_all[:, :, t * 128:(t + 1) * 128], [("hT_all", t)], [("hTtB", b)])
                    else:
                        DMA(hTt[b][:, :, 0:NS], hT_all[:, :, T:T + NS], [("hT_all", NT)], [("hTtB", b)])
                        DMA(ssTt[:], ssT_d[:, :, :], [("hT_all", NT + 1)], ["ssTt"])

                S.record()
                loadB(0)
                loadB(1)
                preB_s1(0)
                for t in range(NT + 1):
                    if t + 2 <= NT:
                        loadB(t + 2)
                    if t + 1 <= NT:
                        preB_s1(t + 1)
                    preB_s2(t)
                S.flush()
            S.barrier()

        if upto >= "B":
          with contextlib.ExitStack() as st:
            sb, ps = mk(st)
            lnwbc = sb("lnwbc", [128, 1024])
            lnbbc = sb("lnbbc", [128, 1024])
            mask3 = sb("mask3_sb", [128, 3, 128])
            nmSU = sb("nmSU", [128, 128])
            nmSL = sb("nmSL", [128, 128])
            DMA(lnwbc[:], rw_ln_w.partition_broadcast(128), [], ["lnwbc"])
            DMA(lnbbc[:], rw_ln_b.partition_broadcast(128), [], ["lnbbc"])
            DMA(mask3[:], mask3_d[:, :, :], [], ["mask3"])
            DMA(nmSU[:], masks_d[1], [], ["nmSU"])
            DMA(nmSL[:], masks_d[2], [], ["nmSL"])
            OP("vector", "tensor_scalar_mul", ["nmSU"], ["nmSU"], out=nmSU[:], in0=nmSU[:], scalar1=-1.0)
            OP("vector", "tensor_scalar_mul", ["nmSL"], ["nmSL"], out=nmSL[:], in0=nmSL[:], scalar1=-1.0)
            osb = sb("osb", [128, 1024])
            oc = sb("oc", [128, 1024])
            sq = sb("sqR", [128, 1024])
            yb = sb("ybR", [128, 1024])
            small = sb("smallR", [128, 64])
            stP = contextlib.ExitStack()
            sbP, _psP = mk(stP)
            pg = [_psP("pgR%d" % i, [128, 1024]) for i in range(1)]
            pr = [_psP("prR%d" % i, [128, 512]) for i in range(6)]
            pkB = [sbP("pkB%d" % i, [128, 7168], BF16) for i in range(3)]
            pkF = [sbP("pkF%d" % i, [128, 1032]) for i in range(3)]
            G3s = [sbP("G3_%d" % i, [128, 16, 3, 128], BF16) for i in range(2)]
            PXs = [[sbP("PX%d_%d" % (j, i), [128, 16, 3, 128], BF16) for i in range(2)] for j in range(2)]
            Wn = sbP("Wn", [128, 1024], BF16)
            U = sbP("U", [128, 1024], BF16)
            Hst = sbP("Hst", [128, 8, 64])
            Hbf = sbP("Hbf", [128, 8, 64], BF16)
            HT = sbP("HT", [64, 8, 128])
            OP("vector", "memset", [], ["Hst"], ap=Hst[:], constant=0.0)
            OP("gpsimd", "memset", [], ["Hbf"], ap=Hbf[:], constant=0.0)
            pri = [0]

            def nextpr():
                k = pri[0] % 6
                pri[0] += 1
                return pr[k], ("prR", k)

            def loadR(t):
                b = t % 3
                DMA(pkB[b][:], preB[t], [("preB", t)], [("pkB", b)])
                DMA(pkF[b][:], preF[t], [("preF", t)], [("pkF", b)])

            def gn_post(rows, bonus, bonusk, t):
                R = slice(0, rows)
                v16 = lambda ap: ap.rearrange("p (h f) -> p h f", h=16)
                OP("vector", "tensor_reduce", ["osb"], ["smR0"], out=small[R, 0:16], in_=v16(osb[R, :]), axis=AX.X, op=ALU.add)
                OP("vector", "scalar_tensor_tensor", ["smR0", "osb"], ["oc"], out=v16(oc[R, :]),
                   in0=small[R, 0:16].unsqueeze(2).to_broadcast([rows, 16, 64]), scalar=-1.0 / 64,
                   in1=v16(osb[R, :]), op0=ALU.mult, op1=ALU.add)
                OP("gpsimd", "tensor_tensor", ["oc"], ["sqR"], out=sq[R, :], in0=oc[R, :], in1=oc[R, :], op=ALU.mult)
                OP("vector", "tensor_reduce", ["sqR"], ["smR1"], out=small[R, 16:32], in_=v16(sq[R, :]), axis=AX.X, op=ALU.add)
                rsqrt_chain(small[R, 16:32], "smR1", rows, 1.0 / 64, GN_EPS)
                OP("vector", "tensor_tensor", ["oc", "smR1"], ["oc"], out=v16(oc[R, :]), in0=v16(oc[R, :]),
                   in1=small[R, 16:32].unsqueeze(2).to_broadcast([rows, 16, 64]), op=ALU.mult)
                OP("gpsimd", "tensor_tensor", ["oc", "lnwbc"], ["oc"], out=oc[R, :], in0=oc[R, :], in1=lnwbc[R, :], op=ALU.mult)
                OP("gpsimd", "tensor_tensor", ["oc", "lnbbc"], ["oc"], out=oc[R, :], in0=oc[R, :], in1=lnbbc[R, :], op=ALU.add)
                OP("gpsimd", "tensor_tensor", ["oc", bonusk], ["ybR"], out=yb[R, :], in0=oc[R, :], in1=bonus, op=ALU.add)
                DMAS(yb_all[t, 0:rows, :], yb[R, :], ["ybR"], [("yb_all", t)])

            def rec_views(t):
                b = t % 3
                pb_, pf_ = pkB[b], pkF[b]
                pbk, pfk = ("pkB", b), ("pkF", b)
                RK = pb_[:, 0:2048].rearrange("p (c j f) -> p c j f", c=8, j=2)
                bT = pb_[:, 2048:3072].rearrange("p (c f) -> p c f", c=8)
                kT = pb_[:, 3072:4096].rearrange("p (c f) -> p c f", c=8)
                btok = pb_[:, 4096:5120]
                ktok = pb_[:, 5120:6144]
                vbf = pb_[:, 6144:7168]
                return pb_, pf_, pbk, pfk, RK, bT, kT, btok, ktok, vbf

            def rec_s1(t):
                q = t % 2
                PX = PXs[q]
                G3 = G3s[q]
                pb_, pf_, pbk, pfk, RK, bT, kT, btok, ktok, vbf = rec_views(t)
                P0 = PX[0]
                for hg in range(4):
                    pn_, pnk = nextpr()
                    for j in range(4):
                        h = hg * 4 + j
                        hh, hp = h // 2, h % 2
                        Pp = slice(hp * 64, hp * 64 + 64)
                        p_, pk_ = nextpr()
                        MM([pbk], [pk_],
                           [dict(out=p_[:, 0:256], lhsT=bT[Pp, hh, :], rhs=RK[Pp, hh, :, :], start=True, stop=True),
                            dict(out=p_[:, 256:512], lhsT=kT[Pp, hh, :], rhs=RK[Pp, hh, :, :], start=True, stop=True)])
                        OP("vector", "tensor_tensor", [pk_, "nmSU"], [("PXP", q, 0, h)], out=P0[:, h, 1, :], in0=p_[:, 0:128],
                           in1=nmSU[:, :], op=ALU.mult)
                        OP("vector", "tensor_tensor", [pk_, "mask3"], [("G3", q, h)], out=G3[:, h, :, :],
                           in0=p_[:, 128:512].rearrange("p (j f) -> p j f", j=3), in1=mask3[:, :, :], op=ALU.mult)
                        MM([pbk], [pnk],
                           [dict(out=pn_[:, j * 128:(j + 1) * 128], lhsT=RK[Pp, hh, 0, :], rhs=bT[Pp, hh, :], start=True, stop=True)])
                    OP("vector", "tensor_tensor", [pnk, "nmSL"], [("PXT", q, 0, hg)], out=P0[:, hg * 4:(hg + 1) * 4, 2, :],
                       in0=pn_[:, :].rearrange("p (j f) -> p j f", j=4),
                       in1=nmSL[:, :].unsqueeze(1).to_broadcast([128, 4, 128]), op=ALU.mult)
                OP("gpsimd", "memset", [], [("PXX", q, 0, h) for h in range(16)], ap=P0[:, :, 0, :], constant=0.0)
                OP("gpsimd", "tensor_tensor", [("PXX", q, 0, h) for h in range(16)] + ["identb"],
                   [("PXX", q, 0, h) for h in range(16)],
                   out=P0[:, :, 0, :], in0=P0[:, :, 0, :],
                   in1=identb[:, :].unsqueeze(1).to_broadcast([128, 16, 128]), op=ALU.add)
                for r in range(7):
                    cur, nxt = PX[r % 2], PX[(r + 1) % 2]
                    ci, ni = r % 2, (r + 1) % 2
                    for g in range(4):
                        hs = list(range(4 * g, 4 * g + 4))
                        rd = [("PXP", q, ci, h) for h in hs] + [("PXX", q, ci, h) for h in hs] + [("PXT", q, ci, g)]
                        xb, xbk = nextpr()
                        MM(rd, [xbk],
                           [dict(out=xb[:, j * 128:(j + 1) * 128], lhsT=cur[:, h, 2, :], rhs=cur[:, h, 0, :], start=True, stop=True)
                            for j, h in enumerate(hs)])
                        if r < 6:
                            for half in range(2):
                                pb2, pb2k = nextpr()
                                h0 = 4 * g + 2 * half
                                mms = []
                                for j in range(2):
                                    h = h0 + j
                                    mms.append(dict(out=pb2[:, j * 256:j * 256 + 128], lhsT=cur[:, h, 2, :], rhs=cur[:, h, 1, :],
                                                    start=True, stop=True))
                                    mms.append(dict(out=pb2[:, j * 256 + 128:(j + 1) * 256], lhsT=cur[:, h, 1, :], rhs=cur[:, h, 2, :],
                                                    start=True, stop=True))
                                MM(rd, [pb2k], mms)
                                OP("scalar", "copy", [pb2k], [("PXP", q, ni, h0), ("PXP", q, ni, h0 + 1), ("PXT", q, ni, g)],
                                   out=nxt[:, h0:h0 + 2, 1:3, :],
                                   in_=pb2[:, :].rearrange("p (a j f) -> p a j f", a=2, j=2))
                        OP("vector", "tensor_tensor", [xbk] + [("PXX", q, ci, h) for h in hs], [("PXX", q, ni, h) for h in hs],
                           out=nxt[:, 4 * g:4 * g + 4, 0, :], in0=xb[:, :].rearrange("p (j f) -> p j f", j=4),
                           in1=cur[:, 4 * g:4 * g + 4, 0, :], op=ALU.add)

            def rec_s2(t):
                q = t % 2
                PX = PXs[q]
                G3 = G3s[q]
                pb_, pf_, pbk, pfk, RK, bT, kT, btok, ktok, vbf = rec_views(t)
                pW, pWk = pg[0], ("pgR", 0)
                mms = []
                for h in range(16):
                    hh, hp = h // 2, h % 2
                    Pp = slice(hp * 64, hp * 64 + 64)
                    mms.append(dict(out=pW[:, h * 64:(h + 1) * 64], lhsT=RK[Pp, hh, 0, :], rhs=Hbf[Pp, hh, :], start=True, stop=False))
                    mms.append(dict(out=pW[:, h * 64:(h + 1) * 64], lhsT=G3[:, h, 1, :], rhs=vbf[:, h * 64:(h + 1) * 64], start=False, stop=True))
                MM([pbk, "Hbf"] + [("G3", q, h) for h in range(16)], [pWk], mms)
                OP("scalar", "mul", [pWk], ["Wn"], out=Wn[:], in_=pW[:, :], mul=-1.0)
                Xf = PX[1]
                xk = [("PXX", q, 1, h) for h in range(16)]
                pU, pUk = pg[0], ("pgR", 0)
                MM(xk + ["Wn"], [pUk],
                   [dict(out=pU[:, h * 64:(h + 1) * 64], lhsT=Xf[:, h, 0, :], rhs=Wn[:, h * 64:(h + 1) * 64], start=True, stop=True)
                    for h in range(16)])
                OP("scalar", "copy", [pUk], ["U"], out=U[:], in_=pU[:, :])
                pO, pOk = pg[0], ("pgR", 0)
                mms = []
                for h in range(16):
                    hh, hp = h // 2, h % 2
                    Pp = slice(hp * 64, hp * 64 + 64)
                    o_ = pO[:, h * 64:(h + 1) * 64]
                    mms.append(dict(out=o_, lhsT=RK[Pp, hh, 1, :], rhs=Hbf[Pp, hh, :], start=True, stop=False))
                    mms.append(dict(out=o_, lhsT=G3[:, h, 0, :], rhs=U[:, h * 64:(h + 1) * 64], start=False, stop=False))
                    mms.append(dict(out=o_, lhsT=G3[:, h, 2, :], rhs=vbf[:, h * 64:(h + 1) * 64], start=False, stop=True))
                MM([pbk, "Hbf", "U"] + [("G3", q, h) for h in range(16)], [pOk], mms)
                OP("scalar", "copy", [pOk], ["osb"], out=osb[:, :], in_=pO[:, :])
                gn_post(128, pf_[:, 0:1024], pfk, t)
                pH, pHk = pg[0], ("pgR", 0)
                pHv = pH[:, :].rearrange("p (c f) -> p c f", c=8)
                mms = []
                for hh in range(8):
                    mms.append(dict(out=pHv[:, hh, :], lhsT=btok[:, hh * 128:(hh + 1) * 128], rhs=U[:, hh * 128:(hh + 1) * 128], start=True, stop=False))
                    mms.append(dict(out=pHv[:, hh, :], lhsT=ktok[:, hh * 128:(hh + 1) * 128], rhs=vbf[:, hh * 128:(hh + 1) * 128], start=False, stop=True))
                MM([pbk, "U"], [pHk], mms)
                for hp in range(2):
                    Pp = slice(hp * 64, hp * 64 + 64)
                    OP("vector", "tensor_tensor", [pHk, "Hst"], ["Hst"], out=Hst[Pp, :, :], in0=Hst[Pp, :, :],
                       in1=pHv[Pp, :, hp * 64:(hp + 1) * 64], op=ALU.add)
                OP("vector", "tensor_tensor", ["Hst", pfk], ["Hst"], out=Hst[:], in0=Hst[:],
                   in1=pf_[:, 1024:1032].unsqueeze(2).to_broadcast([128, 8, 64]), op=ALU.mult)
                OP("gpsimd", "tensor_copy", ["Hst"], ["Hbf"], out=Hbf[:], in_=Hst[:])

            S.record()
            loadR(0)
            loadR(1)
            rec_s1(0)
            for t in range(NT):
                if t + 2 < NT:
                    loadR(t + 2)
                if t + 1 < NT:
                    rec_s1(t + 1)
                rec_s2(t)
            pHT = pg[0][0:64, :].rearrange("p (c f) -> p c f", c=8)
            TR(["Hst", "identf"], [("pgR", 0)],
               [dict(out=pHT[:, hh, :], in_=Hst[:, hh, :], identity=identf[:, :]) for hh in range(8)])
            OP("vector", "tensor_copy", [("pgR", 0)], ["HT"], out=HT[:], in_=pHT)
            DMA(rwkv_p.rearrange("(hh hp) v c -> v hh hp c", hp=2), HT[:].rearrange("v hh (hp c) -> v hh hp c", hp=2),
                ["HT"], ["o_rwkv_p"])
            S.flush()
            S.barrier()
            stP.close()
            S.record()

            pg = [ps("pgS%d" % i, [128, 1024]) for i in range(2)]
            pr = [ps("prS%d" % i, [128, 512]) for i in range(2)]
            samt = sb("samt", [NS, 6, 1024])
            bon_s = sb("bon_s", [NS, 1024])
            sel2 = sb("sel2_sb", [NS, 8, 128])
            hm = sb("hm_sb", [NS, 128])
            Em = sb("Em_sb", [NS, 8])
            Fm = sb("Fm_sb", [128, 64])
            hmask = sb("hmask_sb", [128, 2])
            vdm = sb("vdm", [NS, 16, 128])
            vfm = sb("vfm", [128, 16, 8])
            ofm = sb("ofm", [128, 16, 8])
            ofmm = sb("ofmm", [128, 16, 2, 8])
            St = [sb("St%d" % i, [128, 16, 64]) for i in range(2)]
            Sn = [sb("Sn%d" % i, [128, 16, 64]) for i in range(2)]
            tA = sb("tA", [128, 16, 64])
            tB = sb("tB", [128, 16, 64])
            sk = sb("sk_s", [128, 16])
            DMA(samt[:], sampre[:, 0:6, :], [("sampre", i) for i in range(6)], ["samt"])
            DMA(bon_s[:], preF[NT, 0:NS, 0:1024], [("preF", NT)], ["bon_s"])
            DMA(sel2[:], sel2_d[:, :, :], [], ["sel2"])
            DMA(hm[:], hm_d[:, :], [], ["hm"])
            DMA(Em[:], Em_d[:, :], [], ["Em"])
            DMA(Fm[:], Fm_d[:, :], [], ["Fm"])
            DMA(hmask[:], hmask_d[:, :], [], ["hmask"])
            OP("gpsimd", "tensor_tensor", ["samt", "hm"], ["vdm"], out=vdm[:].rearrange("p h (j f) -> p h j f", j=2),
               in0=samt[:, 5, :].rearrange("p (h f) -> p h f", h=16).unsqueeze(2).to_broadcast([NS, 16, 2, 64]),
               in1=hm[:, :].rearrange("p (j f) -> p j f", j=2).unsqueeze(1).to_broadcast([NS, 16, 2, 64]), op=ALU.mult)
            MM(["vdm", "Em"], [("prR", 0)],
               [dict(out=pr[0][:, h * 8:(h + 1) * 8], lhsT=vdm[:, h, :], rhs=Em[:, :], start=True, stop=True) for h in range(16)])
            OP("vector", "tensor_copy", [("prR", 0)], ["vfm"], out=vfm[:], in_=pr[0][:, 0:128].rearrange("p (h f) -> p h f", h=16))

            def load_st(bbi):
                i = bbi % 2
                DMA(St[i][0:64, :, :], srw[bbi].rearrange("h v k -> v h k"), [], [("St", i)])
                DMA(St[i][64:128, :, :], srw[8 + bbi].rearrange("h v k -> v h k"), [], [("St", i)])

            pgi = [0]

            def bcast(q, bbi):
                k = pgi[0] % 2
                pgi[0] += 1
                MM(["sel2", "samt"], [("pgS", k)],
                   [dict(out=pg[k][:, j * 512:(j + 1) * 512], lhsT=sel2[:, bbi, :], rhs=samt[:, q, j * 512:(j + 1) * 512],
                         start=True, stop=True) for j in range(2)])
                return pg[k][:, :].rearrange("p (h f) -> p h f", h=16), ("pgS", k)

            load_st(0)
            for bbi in range(8):
                i = bbi % 2
                if bbi + 1 < 8:
                    load_st(bbi + 1)
                stk, snk = ("St", i), ("Sn", i)
                pk_, pkk = bcast(0, bbi)
                OP("vector", "tensor_tensor", [stk, pkk], ["tA"], out=tA[:], in0=St[i][:], in1=pk_, op=ALU.mult)
                OP("vector", "tensor_reduce", ["tA"], ["sk_s"], out=sk[:], in_=tA[:], axis=AX.X, op=ALU.add)
                pw_, pwk = bcast(1, bbi)
                OP("vector", "tensor_tensor", [stk, pwk], [snk], out=Sn[i][:], in0=St[i][:], in1=pw_, op=ALU.mult)
                pb_, pbk = bcast(2, bbi)
                OP("vector", "tensor_tensor", [pbk, "sk_s"], ["tB"], out=tB[:], in0=pb_,
                   in1=sk[:, :].unsqueeze(2).to_broadcast([128, 16, 64]), op=ALU.mult)
                OP("vector", "tensor_tensor", [snk, "tB"], [snk], out=Sn[i][:], in0=Sn[i][:], in1=tB[:], op=ALU.subtract)
                pk2_, pk2k = bcast(3, bbi)
                OP("vector", "tensor_tensor", [pk2k, "vfm"], ["tA"], out=tA[:], in0=pk2_,
                   in1=vfm[:, :, bbi:bbi + 1].to_broadcast([128, 16, 64]), op=ALU.mult)
                OP("vector", "tensor_tensor", [snk, "tA"], [snk], out=Sn[i][:], in0=Sn[i][:], in1=tA[:], op=ALU.add)
                DMAS(rwkv_s[bbi].rearrange("h v k -> v h k"), Sn[i][0:64, :, :], [snk], [("o_rwkv_s", bbi)])
                DMAS(rwkv_s[8 + bbi].rearrange("h v k -> v h k"), Sn[i][64:128, :, :], [snk], [("o_rwkv_s", 8 + bbi)])
                pr_, prk = bcast(4, bbi)
                OP("vector", "tensor_tensor", [snk, prk], ["tB"], out=tB[:], in0=Sn[i][:], in1=pr_, op=ALU.mult)
                OP("vector", "tensor_reduce", ["tB"], ["ofm"], out=ofm[:, :, bbi], in_=tB[:], axis=AX.X, op=ALU.add)
            OP("vector", "tensor_tensor", ["ofm", "hmask"], ["ofmm"], out=ofmm[:],
               in0=ofm[:].unsqueeze(2).to_broadcast([128, 16, 2, 8]),
               in1=hmask[:, :].unsqueeze(1).unsqueeze(3).to_broadcast([128, 16, 2, 8]), op=ALU.mult)
            for half in range(2):
                MM(["ofmm", "Fm"], [("prR", half)],
                   [dict(out=pr[half][0:NS, j * 64:(j + 1) * 64], lhsT=ofmm[:, half * 8 + j, :, :].rearrange("p a b -> p (a b)"),
                         rhs=Fm[:, :], start=True, stop=True) for j in range(8)])
                OP("scalar", "copy", [("prR", half)], ["osb"], out=osb[0:NS, half * 512:(half + 1) * 512], in_=pr[half][0:NS, :])
            gn_post(NS, bon_s[:, :], "bon_s", NT)
            S.flush()
          S.barrier()


        if upto >= "C":
          with contextlib.ExitStack() as stC:
            sbC, psC = mk(stC)
            wC = sbC("wC", [128, 8, 3072], BF16)
            wda = sbC("wda", [128, 16, 1024], BF16)
            wdb = sbC("wdb", [128, 8, 1024], BF16)
            wout = sbC("wout", [128, 8, 1024], BF16)
            wpg = sbC("wpg", [128, 8, 1024], BF16)
            wple = sbC("wple", [128, 2, 1024], BF16)
            pgbc = sbC("pgbc", [128, 1024])
            fgbc = sbC("fgbc", [128, 1024])
            DMA(pgbc[:], ple_norm_g.partition_broadcast(128), [], ["pgbc"])
            DMA(fgbc[:], final_norm_g.partition_broadcast(128), [], ["fgbc"])
            with contextlib.ExitStack() as st:
                sb, ps = mk(st)
                stg = [sb("stgC%d" % i, [128, 2048]) for i in range(6)]
                load_w(wC, w_in[:, O5:NCOLS], D, 3072, stg, "wC")
                load_w(wda, w_down_a[:, :], 2048, 1024, stg, "wda")
                load_w(wdb, w_down_b[:, :], 1024, 1024, stg, "wdb")
                load_w(wout, w_out[:, :], 1024, 1024, stg, "wout")
                load_w(wpg, w_ple_gate[:, :], 1024, 1024, stg, "wpg")
                load_w(wple, w_ple[:, :], 256, 1024, stg, "wple")
            S.barrier()
            with contextlib.ExitStack() as st:
                sb, ps = mk(st)
                hTt = [sb("hTtC%d" % i, [128, 8, 128], BF16) for i in range(2)]
                yaT2 = [sb("yaTC%d" % i, [128, 16, 128], BF16) for i in range(2)]
                egs = [[sb("egC%d_%d" % (j, i), [128, 1024], BF16) for i in range(3)] for j in range(2)]
                junkC = sb("junkC", [128, 1024], BF16)
                ybt = sb("ybtC", [128, 1024])
                xt = [sb("xtC%d" % i, [128, 1024]) for i in range(2)]
                pt = [sb("ptC%d" % i, [128, 256]) for i in range(2)]
                f = {0: sb("fC0", [128, 1024]), 3: sb("fC3", [128, 1024])}
                ybg = sb("ybg", [128, 1024], BF16)
                ybT = sb("ybT", [128, 8, 128], BF16)
                mb = sb("mbC", [128, 1024], BF16)
                mT = sb("mTC", [128, 8, 128], BF16)
                hn = sb("hnC", [128, 1024], BF16)
                hnT = sb("hnTC", [128, 8, 128], BF16)
                pbf = sb("pbfC", [128, 256], BF16)
                ppT = sb("ppTC", [128, 2, 128], BF16)
                ssc = sb("sscC", [128, 2])
                pgC = [ps("pgC%d" % i, [128, 1024]) for i in range(3)]
                pTC = [ps("pTC%d" % i, [128, 512]) for i in range(2)]
                pTv = [p_[:, :].bitcast(BF16).rearrange("p (c f) -> p c f", c=8) for p_ in pTC]
                FK = [("fC", i) for i in range(4)]
                PG = [("pgC", i) for i in range(3)]
                tci = [0]

                def transp8(src, srck, rows, dst, dstk, n=8):
                    k = tci[0] % 2
                    tci[0] += 1
                    TR([srck, "identb"], [("pTC", k)],
                       [dict(out=pTv[k][:, c, :rows], in_=src[:rows, c * 128:(c + 1) * 128], identity=identb[:rows, :rows])
                        for c in range(n)])
                    OP("scalar", "copy", [("pTC", k)], [dstk], out=dst[:, 0:n, :rows], in_=pTv[k][:, 0:n, :rows])

                def loadC(t):
                    b = t % 2
                    if t < NT:
                        DMA(hTt[b][:], hT_all[:, :, t * 128:(t + 1) * 128], [], [("hTtC", b)])
                        DMA(xt[b][:], x[t * 128:(t + 1) * 128, :], [], [("xtC", b)])
                        DMA(pt[b][:], pp[t * 128:(t + 1) * 128, :], [], [("ptC", b)])
                    else:
                        DMA(hTt[b][:, :, 0:NS], hT_all[:, :, T:T + NS], [], [("hTtC", b)])
                        DMA(xt[b][0:NS, :], xs[:, :], [], [("xtC", b)])
                        DMA(pt[b][0:NS, :], psm[:, :], [], [("ptC", b)])

                def rms(src, srck, rows, col, gb, gbk, dst, dstk):
                    R = slice(0, rows)
                    OP("vector", "memset", [], [("sscC", col)], ap=ssc[R, col:col + 1], constant=0.0)
                    OP("scalar", "activation", [srck], ["junkC", ("sscC", col)], out=junkC[R, :], in_=src[R, :],
                       func=AF.Square, accum_out=ssc[R, col:col + 1])
                    rsqrt_chain(ssc[R, col:col + 1], ("sscC", col), rows, 1.0 / D, EPS)
                    OP("vector", "scalar_tensor_tensor", [srck, ("sscC", col), gbk], [dstk], out=dst[R, :], in0=src[R, :],
                       scalar=ssc[R, col:col + 1], in1=gb[R, :], op0=ALU.mult, op1=ALU.mult)

                def compC(t):
                    b = t % 2
                    rows = 128 if t < NT else NS
                    R = slice(0, rows)
                    hk = ("hTtC", b)
                    xk = ("xtC", b)
                    yaT = yaT2[b]
                    YAT = ("yaTC", b)
                    eg = egs[b]
                    EG = [("egC", b, i) for i in range(3)]
                    if t < NT:
                        DMA(yaT[:], yaT_all[t], [], [YAT])
                        DMA(ybt[:], yb_all[t], [], ["ybtC"])
                    else:
                        DMA(yaT[:, :, 0:NS], yaT_all[NT, :, :, 0:NS], [], [YAT])
                        DMA(ybt[0:NS, :], yb_all[NT, 0:NS, :], [], ["ybtC"])
                    for gi, (fn, fi) in enumerate(((AF.Silu, 0), (AF.Sigmoid, 1), (AF.Sigmoid, 2))):
                        for j in range(2):
                            col = gi * 1024 + j * 512
                            MM([hk, "wC"], [PG[gi]],
                               [dict(out=pgC[gi][R, j * 512:(j + 1) * 512], lhsT=hTt[b][:, kc, :rows],
                                     rhs=wC[:, kc, col:col + 512], start=(kc == 0), stop=(kc == 7)) for kc in range(8)])
                        OP("scalar", "activation", [PG[gi]], [EG[fi]], out=eg[fi][R, :], in_=pgC[gi][R, :], func=fn)
                    OP("vector", "tensor_tensor", ["ybtC", EG[0]], ["ybg"], out=ybg[R, :], in0=ybt[R, :], in1=eg[0][R, :], op=ALU.mult)
                    transp8(ybg, "ybg", rows, ybT, "ybT")
                    for j in range(2):
                        MM([YAT, "wda"], [PG[0]],
                           [dict(out=pgC[0][R, j * 512:(j + 1) * 512], lhsT=yaT[:, kc, :rows], rhs=wda[:, kc, j * 512:(j + 1) * 512],
                                 start=(kc == 0), stop=(kc == 15)) for kc in range(16)])
                    for j in range(2):
                        MM(["ybT", "wdb"], [PG[1]],
                           [dict(out=pgC[1][R, j * 512:(j + 1) * 512], lhsT=ybT[:, kc, :rows], rhs=wdb[:, kc, j * 512:(j + 1) * 512],
                                 start=(kc == 0), stop=(kc == 7)) for kc in range(8)])
                    OP("vector", "tensor_tensor", [PG[0], EG[1]], [FK[3]], out=f[3][R, :], in0=pgC[0][R, :], in1=eg[1][R, :], op=ALU.mult)
                    OP("vector", "tensor_tensor", [PG[1], EG[2]], [FK[0]], out=f[0][R, :], in0=pgC[1][R, :], in1=eg[2][R, :], op=ALU.mult)
                    OP("vector", "tensor_tensor", [FK[3], FK[0]], ["mbC"], out=mb[R, :], in0=f[3][R, :], in1=f[0][R, :], op=ALU.add)
                    transp8(mb, "mbC", rows, mT, "mTC")
                    for j in range(2):
                        MM(["mTC", "wout"], [PG[2]],
                           [dict(out=pgC[2][R, j * 512:(j + 1) * 512], lhsT=mT[:, kc, :rows], rhs=wout[:, kc, j * 512:(j + 1) * 512],
                                 start=(kc == 0), stop=(kc == 7)) for kc in range(8)])
                    OP("vector", "tensor_tensor", [PG[2], xk], [xk], out=xt[b][R, :], in0=xt[b][R, :], in1=pgC[2][R, :], op=ALU.add)
                    rms(xt[b], xk, rows, 0, pgbc, "pgbc", hn, "hnC")
                    transp8(hn, "hnC", rows, hnT, "hnTC")
                    for j in range(2):
                        MM(["hnTC", "wpg"], [PG[0]],
                           [dict(out=pgC[0][R, j * 512:(j + 1) * 512], lhsT=hnT[:, kc, :rows], rhs=wpg[:, kc, j * 512:(j + 1) * 512],
                                 start=(kc == 0), stop=(kc == 7)) for kc in range(8)])
                    OP("scalar", "activation", [PG[0]], [FK[0]], out=f[0][R, :], in_=pgC[0][R, :], func=AF.Sigmoid)
                    OP("gpsimd", "tensor_copy", [("ptC", b)], ["pbfC"], out=pbf[R, :], in_=pt[b][R, :])
                    transp8(pbf, "pbfC", rows, ppT, "ppTC", n=2)
                    for j in range(2):
                        MM(["ppTC", "wple"], [PG[1]],
                           [dict(out=pgC[1][R, j * 512:(j + 1) * 512], lhsT=ppT[:, kc, :rows], rhs=wple[:, kc, j * 512:(j + 1) * 512],
                                 start=(kc == 0), stop=(kc == 1)) for kc in range(2)])
                    OP("vector", "tensor_tensor", [PG[1], FK[0]], [FK[0]], out=f[0][R, :], in0=pgC[1][R, :], in1=f[0][R, :], op=ALU.mult)
                    OP("vector", "tensor_tensor", [xk, FK[0]], [xk], out=xt[b][R, :], in0=xt[b][R, :], in1=f[0][R, :], op=ALU.add)
                    rms(xt[b], xk, rows, 1, fgbc, "fgbc", xt[b], xk)
                    if t < NT:
                        DMAS(y_p[t * 128:(t + 1) * 128, :], xt[b][:, :], [xk], [("o_y", t)])
                    else:
                        DMA(y_s[:, :], xt[b][0:NS, :], [xk], [("o_y", t)])

                S.record()
                loadC(0)
                for t in range(NT + 1):
                    if t + 1 <= NT:
                        loadC(t + 1)
                    compC(t)
                S.flush()
          S.barrier()

        S.final_wait("sync")

        with nc.Block() as block:
            @block.sync
            def _(e):
                S.emit("sync", e)
                for (wk, val) in S.final[1]:
                    e.wait_ge(S.semof(wk), val)

            @block.scalar
            def _(e):
                S.emit("scalar", e)

            @block.vector
            def _(e):
                S.emit("vector", e)

            @block.gpsimd
            def _(e):
                S.emit("gpsimd", e)

            @block.tensor
            def _(e):
                S.emit("tensor", e)
    return nc


def make_consts():
    c = {}
    c["identb"] = np.eye(128, dtype=np.float32).astype(ml_dtypes.bfloat16)
    c["identf"] = np.eye(128, dtype=np.float32)
    half = 128
    inv = (np.float32(10000.0) ** (-np.arange(half, dtype=np.float32) / np.float32(half))).astype(np.float32)
    pos = np.concatenate([np.arange(T, dtype=np.float32), np.full(128, float(PAST), np.float32)])
    ang = (pos[:, None] * inv[None, :]).astype(np.float32).astype(np.float64)
    cs = np.stack([np.cos(ang), np.sin(ang)], axis=1).astype(np.float32)
    c["ropeq"] = np.ascontiguousarray(cs.reshape(NT + 1, 128, 2, 128))
    c["ropek"] = np.ascontiguousarray((cs.astype(np.float64) / 16.0).astype(np.float32).reshape(NT + 1, 128, 2, 128))
    i = np.arange(128, dtype=np.float64)
    g = np.array(G_RET, dtype=np.float64)
    dec = np.concatenate([g[None, :] ** (i[:, None] + 1.0), g[None, :] ** (-(i[:, None] + 1.0)),
                          g[None, :] ** (127.0 - i[:, None])], axis=1)
    c["dec"] = dec.astype(np.float32)
    p = np.arange(128)[:, None]
    f = np.arange(128)[None, :]
    c["masks"] = np.stack([(p <= f), (p < f), (p > f)]).astype(np.float32)
    sel = np.zeros((NS, NS, 128), np.float32)
    for b in range(NS):
        sel[b, b, :] = 1.0
    c["sel"] = sel
    c0 = -np.exp(-0.5)
    c["tri"] = (np.stack([(p <= f), (p < f)]).astype(np.float64) * c0).astype(np.float32)
    c["c0col"] = np.full((128, 1), c0, np.float32)
    c["mask3"] = np.ascontiguousarray(np.stack([(p <= f), (p < f), (p <= f)], axis=1).astype(np.float32))
    bI = np.arange(NS)[:, None, None]
    bbI = np.arange(8)[None, :, None]
    pI = np.arange(128)[None, None, :]
    c["sel2"] = (bI == (pI // 64) * 8 + bbI).astype(np.float32)
    c["hm"] = ((np.arange(NS)[:, None] // 8) == (np.arange(128)[None, :] // 64)).astype(np.float32)
    c["Em"] = ((np.arange(NS)[:, None] % 8) == np.arange(8)[None, :]).astype(np.float32)
    c["Fm"] = ((np.arange(128)[:, None] % 64) == np.arange(64)[None, :]).astype(np.float32)
    c["hmask"] = ((np.arange(128)[:, None] // 64) == np.arange(2)[None, :]).astype(np.float32)
    c["eye16"] = np.eye(16, dtype=np.float32).reshape(1, 256).astype(ml_dtypes.bfloat16)
    return c


def kernel(**inputs):
    inp = {k: np.asarray(v) for k, v in inputs.items()}
    nc = build_nc()
    consts = make_consts()
    in_maps = []
    for c in range(NCORES):
        m = dict(consts)
        sl = slice(c * NS, (c + 1) * NS)
        m["x"] = np.ascontiguousarray(inp["x_prompt"][c])
        m["xs"] = np.ascontiguousarray(inp["x_sample"][sl, 0, :])
        m["sshift"] = np.ascontiguousarray(inp["state_shift"][0, sl, :])
        m["norm_g"] = np.ascontiguousarray(inp["norm_g"])
        m["w_in"] = np.ascontiguousarray(inp["w_in"][0])
        m["sret"] = np.ascontiguousarray(inp["state_ret"][0, sl])
        m["srw"] = np.ascontiguousarray(inp["state_rwkv"][0, sl])
        for nm in ("rw_mu", "rw_w0", "rw_a0", "rw_k_k", "rw_k_a", "rw_ln_w", "rw_ln_b"):
            m[nm] = np.ascontiguousarray(inp[nm].reshape(1, -1))
        m["rw_r_k"] = np.ascontiguousarray(inp["rw_r_k"].reshape(1, 1024))
        for nm in ("w_down_a", "w_down_b", "w_out", "w_ple_gate", "w_ple"):
            m[nm] = np.ascontiguousarray(inp[nm][0])
        m["ple_norm_g"] = np.ascontiguousarray(inp["ple_norm_g"].reshape(1, -1))
        m["final_norm_g"] = np.ascontiguousarray(inp["final_norm_g"].reshape(1, -1))
        m["pp"] = np.ascontiguousarray(inp["p_prompt"][0, c])
        m["psm"] = np.ascontiguousarray(inp["p_sample"][0, sl, 0, :])
        m["rw_w2"] = np.ascontiguousarray(inp["rw_w2"][0])
        m["rw_a2"] = np.ascontiguousarray(inp["rw_a2"][0])
        in_maps.append(m)
    res = run_bass_kernel_spmd(nc, in_maps, core_ids=list(range(NCORES)))
    R = res.results
    shift_p = np.stack([R[c]["shift_p"][0] for c in range(NCORES)])[None]
    shift_s = np.concatenate([R[c]["shift_s"] for c in range(NCORES)], axis=0)[None]
    ret_p = np.stack([R[c]["ret_p"] for c in range(NCORES)])[None]
    ret_s = np.concatenate([R[c]["ret_s"] for c in range(NCORES)], axis=0)[None]
    if DEBUG:
        for nm in ("preB", "preF", "yb_all"):
            DBG[nm] = R[0][nm]
    rwkv_p = np.stack([R[c]["rwkv_p"] for c in range(NCORES)])[None]
    rwkv_s = np.concatenate([R[c]["rwkv_s"] for c in range(NCORES)], axis=0)[None]
    y_p = np.stack([R[c]["y_p"] for c in range(NCORES)])
    y_s = np.concatenate([R[c]["y_s"] for c in range(NCORES)], axis=0)[:, None, :]
    return y_p, y_s, ret_p, rwkv_p, shift_p, ret_s, rwkv_s, shift_s
```
